# Optimizing a Trainium2 kernel written in Bass

```python
import jax, jax.numpy as jnp
from jax import lax
import numpy as np

D_MODEL = 1024
BATCH = 8
SEQ = 4096
DEPTH = 1

GRID_W = 64
CTX_LEN = 256
N_HEADS = 4
D_KEY = D_MODEL // 2
D_VAL = D_MODEL
HEAD_K = D_KEY // N_HEADS
HEAD_V = D_VAL // N_HEADS
DECAY_RANK = 16
DECAY_TAU = 16.0
CHUNK = 64
D_CONV = D_MODEL
CONV_W = 3
D_FF = ((-(-8 * D_MODEL // 3)) + 255) // 256 * 256
EPS = 1e-6
IN_SPLITS = (D_KEY, D_KEY, D_VAL, D_VAL, 2 * DECAY_RANK, D_CONV, D_CONV, D_CONV, D_MODEL, D_MODEL)
N_IN = sum(IN_SPLITS)

kernel_name = "hybrid_gla_shortconv_dit_block"


def rmsnorm(x, g):
    xf = x.astype(jnp.float32)
    y = xf * lax.rsqrt(jnp.mean(xf * xf, axis=-1, keepdims=True) + EPS)
    return (y * g.astype(jnp.float32)).astype(x.dtype)


def modulate(h, shift, scale):
    return h * (1.0 + scale[:, None, :]) + shift[:, None, :]


def gla_chunked(q, k, v, log_a, s0):
    Bn, H, T, K = q.shape
    V = v.shape[-1]
    N = T // CHUNK
    f = lambda t: t.astype(jnp.float32).reshape(Bn, H, N, CHUNK, t.shape[-1])
    q, k, v, log_a = f(q), f(k), f(v), f(log_a)
    q = q * (K ** -0.5)
    b = jnp.cumsum(log_a, axis=3)
    b_mid = b[:, :, :, CHUNK // 2 - 1:CHUNK // 2, :]
    b_last = b[:, :, :, CHUNK - 1:, :]
    causal = jnp.tril(jnp.ones((CHUNK, CHUNK), dtype=bool))
    scores = jnp.einsum('bhnik,bhnjk->bhnij', q * jnp.exp(b - b_mid), k * jnp.exp(b_mid - b))
    scores = jnp.where(causal, scores, 0.0)
    o_intra = jnp.einsum('bhnij,bhnjv->bhniv', scores, v)
    kv = jnp.einsum('bhnjk,bhnjv->nbhkv', k * jnp.exp(b_last - b), v)
    decay = jnp.moveaxis(jnp.exp(b_last[:, :, :, 0, :]), 2, 0)

    def step(s, inp):
        d, kv_n = inp
        return d[..., None] * s + kv_n, s

    s_final, s_before = lax.scan(step, s0.astype(jnp.float32), (decay, kv))
    o_inter = jnp.einsum('bhnik,nbhkv->bhniv', q * jnp.exp(b), s_before)
    return (o_intra + o_inter).reshape(Bn, H, T, V), s_final


def gla_bidir(q, k, v, la_f, la_b, s0_f, s0_b):
    o_f, s_f = gla_chunked(q, k, v, la_f, s0_f)
    flip = lambda t: jnp.flip(t, axis=2)
    o_b, s_b = gla_chunked(flip(q), flip(k), flip(v), flip(la_b), s0_b)
    return o_f + flip(o_b), s_f, s_b


def project(h, w_in, w_dec_up, b_dec):
    Bn, T, _ = h.shape
    z = h @ w_in
    q, k, v, g, dd, cb, cc, ch, ga, gb = jnp.split(z, np.cumsum(IN_SPLITS)[:-1].tolist(), axis=-1)
    dd = dd.reshape(Bn, T, 2, DECAY_RANK)
    zd = jnp.einsum('btdr,drk->dbtk', dd, w_dec_up) + b_dec[:, None, None, :]
    log_a = jax.nn.log_sigmoid(zd.astype(jnp.float32)) / DECAY_TAU
    heads = lambda t, hd: t.reshape(Bn, T, N_HEADS, hd).transpose(0, 2, 1, 3)
    return (heads(q, HEAD_K), heads(k, HEAD_K), heads(v, HEAD_V), g,
            heads(log_a[0], HEAD_K), heads(log_a[1], HEAD_K), cb, cc, ch, ga, gb)


def seq_conv(u, w):
    pad = CONV_W // 2
    T = u.shape[1]
    up = jnp.pad(u, ((0, 0), (pad, pad), (0, 0)))
    y = up[:, 0:T] * w[0]
    for j in range(1, CONV_W):
        y = y + up[:, j:j + T] * w[j]
    return y


def merge_branches(o, g, cb, cc, ch, ga, gb, rows, g_head, w_gla_out, w_conv, w_conv_out, w_mix_out):
    Bn, H, T, V = o.shape
    o = rmsnorm(o, g_head).transpose(0, 2, 1, 3).reshape(Bn, T, H * V).astype(g.dtype)
    y_gla = (o * jax.nn.silu(g)) @ w_gla_out
    u = cc * ch
    if rows is None:
        uc = seq_conv(u, w_conv)
    else:
        uc = seq_conv(u.reshape(Bn * rows, T // rows, -1), w_conv).reshape(Bn, T, -1)
    y_conv = (cb * uc) @ w_conv_out
    merged = jax.nn.sigmoid(ga) * y_gla + jax.nn.sigmoid(gb) * y_conv
    return merged @ w_mix_out


def swiglu(h, w_in, w_out):
    a, b = jnp.split(h @ w_in, 2, axis=-1)
    return (jax.nn.silu(a) * b) @ w_out


def setup_inputs(seed: int = 0) -> dict:
    key = jax.random.key(seed)
    ks = jax.random.split(key, 20)
    nrm = lambda k, shape, s: jax.random.normal(k, shape, jnp.float32) * s
    return {
        "x": nrm(ks[0], (BATCH, SEQ, D_MODEL), 1.0),
        "c": nrm(ks[1], (BATCH, D_MODEL), 1.0),
        "ctx": nrm(ks[2], (BATCH, CTX_LEN, D_MODEL), 1.0),
        "c_ctx": nrm(ks[3], (D_MODEL,), 1.0),
        "w_ada": nrm(ks[4], (DEPTH, D_MODEL, 6 * D_MODEL), 0.5 * D_MODEL ** -0.5),
        "b_ada": nrm(ks[5], (DEPTH, 6 * D_MODEL), 0.01),
        "g_mix": 1.0 + nrm(ks[6], (DEPTH, D_MODEL), 0.02),
        "w_in": nrm(ks[7], (DEPTH, D_MODEL, N_IN), D_MODEL ** -0.5),
        "w_dec_up": nrm(ks[8], (DEPTH, 2, DECAY_RANK, D_KEY), DECAY_RANK ** -0.5),
        "b_dec": nrm(ks[9], (DEPTH, 2, D_KEY), 0.1),
        "g_head": 1.0 + nrm(ks[10], (DEPTH, HEAD_V), 0.02),
        "w_gla_out": nrm(ks[11], (DEPTH, D_VAL, D_MODEL), D_VAL ** -0.5),
        "w_conv": nrm(ks[12], (DEPTH, CONV_W, D_CONV), CONV_W ** -0.5),
        "w_conv_out": nrm(ks[13], (DEPTH, D_CONV, D_MODEL), D_CONV ** -0.5),
        "w_mix_out": nrm(ks[14], (DEPTH, D_MODEL, D_MODEL), D_MODEL ** -0.5),
        "g_ffn": 1.0 + nrm(ks[15], (DEPTH, D_MODEL), 0.02),
        "w_ffn_in": nrm(ks[16], (DEPTH, D_MODEL, 2 * D_FF), D_MODEL ** -0.5),
        "w_ffn_out": nrm(ks[17], (DEPTH, D_FF, D_MODEL), D_FF ** -0.5),
        "g_final": 1.0 + nrm(ks[18], (D_MODEL,), 0.02),
    }


def reference(x, c, ctx, c_ctx, w_ada, b_ada, g_mix, w_in, w_dec_up, b_dec, g_head, w_gla_out,
              w_conv, w_conv_out, w_mix_out, g_ffn, w_ffn_in, w_ffn_out, g_final):
    rows = x.shape[1] // GRID_W
    bc = ctx.shape[0]
    zeros_state = jnp.zeros((bc, N_HEADS, HEAD_K, HEAD_V), jnp.float32)
    for l in range(DEPTH):
        last = l == DEPTH - 1
        mod_lat = jnp.split(jax.nn.silu(c) @ w_ada[l] + b_ada[l], 6, axis=-1)
        mod_ctx = jnp.split(jax.nn.silu(c_ctx)[None, :] @ w_ada[l] + b_ada[l], 6, axis=-1)
        sh1, sc1, gt1, sh2, sc2, gt2 = mod_lat
        sh1c, sc1c, gt1c, sh2c, sc2c, gt2c = mod_ctx

        pl = project(modulate(rmsnorm(x, g_mix[l]), sh1, sc1), w_in[l], w_dec_up[l], b_dec[l])
        pc = project(modulate(rmsnorm(ctx, g_mix[l]), sh1c, sc1c), w_in[l], w_dec_up[l], b_dec[l])
        o_c, s_cf, s_cb = gla_bidir(pc[0], pc[1], pc[2], pc[4], pc[5], zeros_state, zeros_state)
        o_l, _, _ = gla_bidir(pl[0], pl[1], pl[2], pl[4], pl[5], s_cf, s_cb)
        mix_l = merge_branches(o_l, pl[3], pl[6], pl[7], pl[8], pl[9], pl[10], rows, g_head[l],
                               w_gla_out[l], w_conv[l], w_conv_out[l], w_mix_out[l])
        x = x + gt1[:, None, :] * mix_l
        if not last:
            mix_c = merge_branches(o_c, pc[3], pc[6], pc[7], pc[8], pc[9], pc[10], None, g_head[l],
                                   w_gla_out[l], w_conv[l], w_conv_out[l], w_mix_out[l])
            ctx = ctx + gt1c[:, None, :] * mix_c

        x = x + gt2[:, None, :] * swiglu(modulate(rmsnorm(x, g_ffn[l]), sh2, sc2), w_ffn_in[l], w_ffn_out[l])
        if not last:
            ctx = ctx + gt2c[:, None, :] * swiglu(modulate(rmsnorm(ctx, g_ffn[l]), sh2c, sc2c),
                                                  w_ffn_in[l], w_ffn_out[l])
    return rmsnorm(x, g_final)
```

```python
import numpy as np
import concourse.bass as bass
import concourse.mybir as mybir
from concourse.bass_utils import run_bass_kernel_spmd

F32 = mybir.dt.float32
BF16 = mybir.dt.bfloat16
AF = mybir.ActivationFunctionType
ALU = mybir.AluOpType

D = 1024
DC = 8
TT = 128
NH = 4
HK = 128
HV = 256
NIN = 8224
DFF = 2816
FC = 22
EPS = 1e-6
TAU = 16.0
C_Q, C_K, C_V, C_G, C_DD, C_CB, C_CC, C_CH, C_GA, C_GB = 0, 512, 1024, 2048, 3072, 3104, 4128, 5152, 6176, 7200

ENGS = ("pe", "act", "dve", "pool", "sp")


class Buf:
    __slots__ = ("name", "last_w", "readers")

    def __init__(self, name):
        self.name = name
        self.last_w = None
        self.readers = []


class Op:
    __slots__ = ("eng", "fn", "deps", "is_dma", "dsem", "dma_val", "needs_inc", "sem_val", "idx")


class Prog:
    def __init__(self, nc):
        self.nc = nc
        self.ops = []
        self.dma_counts = {}

    def add(self, eng, fn, reads=(), writes=(), dsem=None):
        op = Op()
        op.eng = eng
        op.fn = fn
        op.is_dma = dsem is not None
        op.dsem = dsem
        op.needs_inc = False
        op.sem_val = None
        op.idx = len(self.ops)
        op.dma_val = 0
        if op.is_dma:
            k = id(dsem)
            self.dma_counts[k] = self.dma_counts.get(k, 0) + 1
            op.dma_val = 16 * self.dma_counts[k]
        deps = {}
        wset = set(id(b) for b in writes)
        for b in reads:
            if b.last_w is not None:
                deps[b.last_w.idx] = b.last_w
        for b in writes:
            if b.last_w is not None:
                deps[b.last_w.idx] = b.last_w
            for r in b.readers:
                deps[r.idx] = r
        fdeps = []
        for d in deps.values():
            if d is op:
                continue
            if (not d.is_dma) and d.eng == "pe" and eng == "pe" and not op.is_dma:
                continue
            if not d.is_dma:
                d.needs_inc = True
            fdeps.append(d)
        op.deps = fdeps
        for b in writes:
            b.last_w = op
            b.readers = []
        for b in reads:
            if id(b) not in wset:
                if not op.is_dma:
                    b.readers = [r for r in b.readers if r.is_dma or r.eng != eng]
                b.readers.append(op)
        self.ops.append(op)
        return op

    def emit(self, sems, final_waits):
        nc = self.nc
        cnt = {e: 0 for e in ENGS}
        for op in self.ops:
            if not op.is_dma and op.needs_inc:
                cnt[op.eng] += 1
                op.sem_val = cnt[op.eng]
        with nc.Block() as block:
            for e in ENGS:
                eops = [o for o in self.ops if o.eng == e]

                def body(eng_handle, eops=eops, e=e):
                    waited = {}
                    for op in eops:
                        need = {}
                        for d in op.deps:
                            if d.is_dma:
                                s, v = d.dsem, d.dma_val
                            else:
                                s, v = sems[d.eng], d.sem_val
                            k = id(s)
                            if k not in need or need[k][1] < v:
                                need[k] = (s, v)
                        for k, (s, v) in need.items():
                            if waited.get(k, 0) >= v:
                                continue
                            eng_handle.wait_ge(s, v)
                            waited[k] = v
                        ins = op.fn(eng_handle)
                        if op.is_dma:
                            ins.then_inc(op.dsem, 16)
                        elif op.needs_inc:
                            ins.then_inc(sems[op.eng], 1)
                    if e == "sp":
                        for s, v in final_waits():
                            eng_handle.wait_ge(s, v)

                {"pe": block.tensor, "act": block.scalar, "dve": block.vector,
                 "pool": block.gpsimd, "sp": block.sync}[e](body)


class Tile:
    __slots__ = ("h", "buf", "off", "nbytes", "name")


class Arena:
    BASE = 16512
    END = 229344

    def __init__(self, nc):
        self.nc = nc
        self.live = []
        self.tomb = []
        self.n = 0

    def alloc(self, name, shape, dtype):
        esz = 4 if dtype == F32 else 2
        nbytes = int(np.prod(shape[1:])) * esz
        nbytes = (nbytes + 63) // 64 * 64
        occ = sorted((o, e) for o, e, _ in self.live)
        cur = self.BASE
        off = None
        for o, e in occ:
            if o - cur >= nbytes:
                off = cur
                break
            cur = max(cur, e)
        if off is None:
            if self.END - cur >= nbytes:
                off = cur
            else:
                raise RuntimeError(f"SBUF arena full allocating {name} {shape}")
        t = Tile()
        self.n += 1
        t.h = self.nc.alloc_sbuf_tensor_at(f"{name}_{self.n}", list(shape), dtype, offset=off)
        t.buf = Buf(name)
        t.off, t.nbytes, t.name = off, nbytes, name
        end = off + nbytes
        keep = []
        for (o, e, ops) in self.tomb:
            if o < end and off < e:
                t.buf.readers.extend(ops)
                if o >= off and e <= end:
                    continue
            keep.append((o, e, ops))
        self.tomb = keep
        self.live.append((off, end, t))
        return t

    def free(self, *tiles):
        for t in tiles:
            self.live = [x for x in self.live if x[2] is not t]
            ops = list(t.buf.readers)
            if t.buf.last_w is not None:
                ops.append(t.buf.last_w)
            if ops:
                self.tomb.append((t.off, t.off + t.nbytes, ops))


def build_program(NT=32, NCT=2, dbg=False):
    nc = bass.Bass("TRN2", target_bir_lowering=False)
    P = Prog(nc)
    AR = Arena(nc)
    sem_list = []

    def newsem(name):
        s = nc.alloc_semaphore(name)
        sem_list.append(s)
        return s

    esems = {e: newsem("e_" + e) for e in ("pe", "act", "dve", "pool")}

    def din(name, shape, dt=F32):
        return nc.dram_tensor(name, list(shape), dt, kind="ExternalInput").ap()

    def dscr(name, shape, dt):
        kind = "ExternalOutput" if dbg else "Internal"
        return nc.dram_tensor(name, list(shape), dt, kind=kind).ap()

    x_t = din("x_t", [NT, 128, DC, TT])
    ctx_t = din("ctx_t", [NCT, 128, DC, TT])
    cvec = din("cvec", [128, DC, 2])
    w_ada = din("w_ada", [D, 6 * D])
    b_ada = din("b_ada_fm", [128, 48])
    g_mix = din("g_mix_fm", [128, DC])
    g_ffn = din("g_ffn_fm", [128, DC])
    g_fin = din("g_fin_fm", [128, DC])
    w_in = din("w_in", [D, NIN])
    wup = din("wup_aug", [33, 1024])
    g_head = din("g_head4", [1, 1024])
    w_go = din("w_gla_out", [D, D])
    w_co = din("w_conv_out", [D, D])
    w_mo = din("w_mix_out", [D, D])
    w_cv = din("w_conv_fm", [128, DC, 3])
    w_fi = din("w_ffn_in", [D, 2 * DFF])
    w_fo = din("w_ffn_out", [DFF, D])
    c_id = din("c_ident", [128, 128])
    c_trf = din("c_trif", [128, 128])
    c_trb = din("c_trib", [128, 128])
    out_t = nc.dram_tensor("out_t", [NT, 128, DC, TT], F32, kind="ExternalOutput").ap()
    hT_s = dscr("hT_s", [NT, 128, DC, TT], BF16)
    ma_s = dscr("ma_s", [NT, 128, DC, TT], F32)
    x1_s = dscr("x1_s", [NT, 128, DC, TT], F32)
    ssb_s = dscr("ssb_s", [NT, 128, NH * HV], BF16)
    v_s = dscr("v_s", [NT, 128, NH * HV], BF16)
    v_b = [Buf(f"v_s{t}") for t in range(NT)]
    ssb_b = [Buf(f"ssb_s{t}") for t in range(NT)]
    dbg_t = {}

    hT_b = [Buf(f"hT_s{t}") for t in range(NT)]
    ma_b = [Buf(f"ma_s{t}") for t in range(NT)]
    x1_b = [Buf(f"x1_s{t}") for t in range(NT)]
    out_sem = newsem("s_out")

    PS = []
    for i in range(8):
        t = Tile()
        t.h = nc.alloc_psum_tensor(f"ps{i}", [128, 512], F32)
        t.buf = Buf(f"ps{i}")
        PS.append(t)

    def psf(i, *shape):
        ap = PS[i].h[:, :]
        n = int(np.prod(shape))
        ap = PS[i].h[:, 0:n]
        if len(shape) == 2:
            ap = ap.rearrange("p (a b) -> p a b", a=shape[0])
        return ap

    def psb(i, *shape):
        n = int(np.prod(shape))
        ap = PS[i].h[:, 0:n // 2].bitcast(BF16)
        if len(shape) == 2:
            ap = ap.rearrange("p (a b) -> p a b", a=shape[0])
        return ap

    def op(eng, meth, reads, writes, *a, **kw):
        return P.add(eng, lambda e: getattr(e, meth)(*a, **kw), [r.buf if isinstance(r, Tile) else r for r in reads],
                     [w.buf if isinstance(w, Tile) else w for w in writes])

    def dma(eng, out, in_, reads, writes, dsem):
        return P.add(eng, lambda e: e.dma_start(out=out, in_=in_), [r.buf if isinstance(r, Tile) else r for r in reads],
                     [w.buf if isinstance(w, Tile) else w for w in writes], dsem=dsem)

    def mm(out, lhsT, rhs, start, stop, reads, writes):
        return op("pe", "matmul", reads, writes, out, lhsT=lhsT, rhs=rhs, start=start, stop=stop)

    def wload(tile, col0, ncols, src, sem, rows_chunks=DC):
        pass

    ident = AR.alloc("ident", [128, 128], BF16)
    trif = AR.alloc("trif", [128, 128], F32)
    trib = AR.alloc("trib", [128, 128], F32)
    ones = AR.alloc("ones", [128, 128], BF16)
    s_c = newsem("s_const")
    dma("pool", ident.h[:, :], c_id, [], [ident], s_c)
    dma("sp", trif.h[:, :], c_trf, [], [trif], newsem("s_trf"))
    dma("sp", trib.h[:, :], c_trb, [], [trib], newsem("s_trb"))
    op("dve", "memset", [], [ones], ones.h[:, :], 1.0)

    x1_pre = [AR.alloc(f"x1_{s}", [128, DC, TT], F32) for s in range(2)]
    vec = AR.alloc("vec", [128, 16, DC], F32)
    V_GMIX, V_GFFN, V_GFIN, V_GS1, V_SH1, V_GS1C, V_SH1C, V_GT1, V_GS2, V_SH2, V_GT2 = range(11)
    s_vec = newsem("s_vec")
    dma("sp", vec.h[:, V_GMIX, :], g_mix, [], [vec], s_vec)
    dma("sp", vec.h[:, V_GFFN, :], g_ffn, [], [vec], s_vec)
    dma("sp", vec.h[:, V_GFIN, :], g_fin, [], [vec], s_vec)
    bada = AR.alloc("bada", [128, 48], F32)
    dma("sp", bada.h[:, :], b_ada, [], [bada], newsem("s_bada"))
    wcv = AR.alloc("wcv", [128, DC, 3], F32)
    dma("sp", wcv.h[:, :, :], w_cv, [], [wcv], newsem("s_wcv"))

    cv = AR.alloc("cv", [128, DC, 2], F32)
    cvb = AR.alloc("cvb", [128, DC, 2], BF16)
    dma("sp", cv.h[:, :, :], cvec, [], [cv], newsem("s_cv"))
    op("act", "activation", [cv], [cvb], out=cvb.h[:, :, :], in_=cv.h[:, :, :], func=AF.Silu)
    wa = [AR.alloc(f"wa{i}", [128, DC, 512], BF16) for i in range(4)]
    wa_s = [newsem(f"s_wa{i}") for i in range(4)]
    waf = [AR.alloc(f"waf{i}", [128, DC, 512], F32) for i in range(2)]
    waf_s = [newsem(f"s_waf{i}") for i in range(2)]
    modp = PS[0]
    for blk in range(12):
        w = wa[blk % 4]
        src_ = w_ada[:, blk * 512:(blk + 1) * 512].rearrange("(c p) n -> p c n", p=128)
        if blk % 2 == 0:
            dma("pool", w.h[:, :, :], src_, [], [w], wa_s[blk % 4])
        else:
            wf = waf[(blk // 2) % 2]
            dma("sp", wf.h[:, :, :], src_, [], [wf], waf_s[(blk // 2) % 2])
            op("dve", "tensor_copy", [wf], [w], out=w.h[:, :, :], in_=wf.h[:, :, :])
        for jj in range(4):
            j = blk * 4 + jj
            for c in range(DC):
                mm(psf(0, 48, 2)[:, j, :], w.h[:, c, jj * 128:(jj + 1) * 128], cvb.h[:, c, :], c == 0, c == DC - 1, [w, cvb], [modp])
    mod = AR.alloc("mod", [128, 48, 2], F32)
    op("dve", "tensor_tensor", [modp, bada], [mod], out=mod.h[:, :, :], in0=psf(0, 48, 2), in1=bada.h[:, :].unsqueeze(2).to_broadcast([128, 48, 2]), op=ALU.add)
    AR.free(*wa, *waf, cv, cvb, bada)

    def modv(k, s):
        return mod.h[:, k * 8:(k + 1) * 8, s]

    def vslot(i):
        return vec.h[:, i, :]

    op("dve", "scalar_tensor_tensor", [mod, vec], [vec], out=vslot(V_GS1), in0=modv(1, 0), scalar=1.0, in1=vslot(V_GMIX), op0=ALU.add, op1=ALU.mult)
    op("dve", "scalar_tensor_tensor", [mod, vec], [vec], out=vslot(V_GS1C), in0=modv(1, 1), scalar=1.0, in1=vslot(V_GMIX), op0=ALU.add, op1=ALU.mult)
    op("dve", "scalar_tensor_tensor", [mod, vec], [vec], out=vslot(V_GS2), in0=modv(4, 0), scalar=1.0, in1=vslot(V_GFFN), op0=ALU.add, op1=ALU.mult)
    op("dve", "tensor_copy", [mod], [vec], out=vslot(V_SH1), in_=modv(0, 0))
    op("dve", "tensor_copy", [mod], [vec], out=vslot(V_SH1C), in_=modv(0, 1))
    op("dve", "tensor_copy", [mod], [vec], out=vslot(V_GT1), in_=modv(2, 0))
    op("dve", "tensor_copy", [mod], [vec], out=vslot(V_SH2), in_=modv(3, 0))
    op("dve", "tensor_copy", [mod], [vec], out=vslot(V_GT2), in_=modv(5, 0))
    if dbg:
        dbg_t["mod"] = nc.dram_tensor("dbg_mod", [128, 48, 2], F32, kind="ExternalOutput").ap()
        dma("sp", dbg_t["mod"], mod.h[:, :, :], [mod], [], newsem("s_dbgmod"))

    def bc_c(ap2, n):
        return ap2.unsqueeze(2).to_broadcast([128, DC, n])

    def interleave(make_gen, order, offset, K=2):
        active = []
        j = 0
        while j < len(order) or active:
            if j < len(order) and all(a[0] != j % K for a in active) and (not active or active[-1][2] >= offset):
                active.append([j % K, make_gen(j, order[j], j % K), 0])
                j += 1
            for a in list(active):
                try:
                    next(a[1])
                    a[2] += 1
                except StopIteration:
                    active.remove(a)

    def gen_hT(xt, hT, gs_slot, sh_slot, bank, sq, rs, tmp):
        op("act", "activation", [xt], [sq], out=sq.h[:, :, :], in_=xt.h[:, :, :], func=AF.Square)
        yield
        for c in range(DC):
            mm(psf(bank, TT), ones.h[:, :], sq.h[:, c, :], c == 0, c == DC - 1, [ones, sq], [PS[bank]])
        yield
        op("act", "activation", [PS[bank]], [rs], out=rs.h[:, :], in_=psf(bank, TT), func=AF.Ln, scale=1.0 / D, bias=EPS)
        op("act", "activation", [rs], [rs], out=rs.h[:, :], in_=rs.h[:, :], func=AF.Exp, scale=-0.5)
        op("dve", "tensor_tensor", [xt, rs], [tmp], out=tmp.h[:, :, :], in0=xt.h[:, :, :], in1=rs.h[:, :].unsqueeze(1).to_broadcast([128, DC, TT]), op=ALU.mult)
        yield
        op("dve", "tensor_tensor", [tmp, vec], [tmp], out=tmp.h[:, :, :], in0=tmp.h[:, :, :], in1=bc_c(vslot(gs_slot), TT), op=ALU.mult)
        op("dve", "tensor_tensor", [tmp, vec], [hT], out=hT.h[:, :, :], in0=tmp.h[:, :, :], in1=bc_c(vslot(sh_slot), TT), op=ALU.add)
        yield

    def decay_dir(R, d, zb, cb, full_q):
        dda, nl, cm = R["dda"], R["nl"], R["cm"]
        mm(psf(zb, 512), dda.h[:, :], wupt.h[:, d * 512:(d + 1) * 512], True, True, [dda, wupt], [PS[zb]])
        op("act", "activation", [PS[zb]], [nl], out=nl.h[:, :], in_=psf(zb, 512), func=AF.Exp, scale=-1.0)
        op("act", "activation", [nl], [nl], out=nl.h[:, :], in_=nl.h[:, :], func=AF.Ln, bias=1.0)
        yield
        tri = trif if d == 0 else trib
        for h in range(NH):
            mm(psf(cb, NH, TT)[:, h, :], nl.h[:, h * HK:(h + 1) * HK], tri.h[:, :], True, True, [nl, tri], [PS[cb]])
        mid = 63 if d == 0 else 64
        last = TT - 1 if d == 0 else 0
        cmid, Eq, Ek, em, dd_ = R["cmid"][d], R["Eq"][d], R["Ek"][d], R["em"][d], R["dd"][d]
        op("dve", "tensor_copy", [PS[cb]], [cmid], out=cmid.h[:, :], in_=psf(cb, NH, TT)[:, :, mid])
        op("dve", "tensor_tensor", [PS[cb], cmid], [cm], out=cm.h[:, :, :], in0=psf(cb, NH, TT), in1=cmid.h[:, :].unsqueeze(2).to_broadcast([128, NH, TT]), op=ALU.subtract)
        if full_q:
            op("act", "activation", [cm], [Eq], out=Eq.h[:, :, :], in_=cm.h[:, :, :], func=AF.Exp, scale=-1.0 / TAU)
        else:
            op("act", "activation", [cm], [Eq], out=Eq.h[:, :, last:last + 1], in_=cm.h[:, :, last:last + 1], func=AF.Exp, scale=-1.0 / TAU)
        op("act", "activation", [cm], [Ek], out=Ek.h[:, :, :], in_=cm.h[:, :, :], func=AF.Exp, scale=1.0 / TAU)
        op("act", "activation", [cmid], [em], out=em.h[:, :], in_=cmid.h[:, :], func=AF.Exp, scale=-1.0 / TAU)
        op("dve", "tensor_tensor", [em, Eq], [dd_], out=dd_.h[:, :], in0=em.h[:, :], in1=Eq.h[:, :, last], op=ALU.mult)

    def state_update(S, kvb0, Eq, dd_, last, kvs):
        for h in range(NH):
            b = kvb0 + h // 2
            op("act", "activation", [PS[b], Eq], [kvs], out=kvs.h[:, h, :], in_=psf(b, 2, HV)[:, h % 2, :], func=AF.Identity, scale=Eq.h[:, h, last:last + 1])
        for h in range(NH):
            op("dve", "scalar_tensor_tensor", [S, dd_, kvs], [S], out=S.h[:, h, :], in0=S.h[:, h, :], scalar=dd_.h[:, h:h + 1], in1=kvs.h[:, h, :], op0=ALU.mult, op1=ALU.add)

    def wsrc(c0, n):
        return w_in[:, c0:c0 + n].rearrange("(c p) n -> p c n", p=128)

    def wload(wt, src2d, ncols, sem_):
        for c0 in range(0, ncols, 512):
            n = min(512, ncols - c0)
            dma("pool", wt.h[:, :, c0:c0 + n], src2d[:, c0:c0 + n].rearrange("(c p) n -> p c n", p=128), [], [wt], sem_)

    S_f = AR.alloc("S_f", [128, NH, HV], F32)
    S_b = AR.alloc("S_b", [128, NH, HV], F32)
    op("dve", "memset", [], [S_f], S_f.h[:, :, :], 0.0)
    op("pool", "memset", [], [S_b], S_b.h[:, :, :], 0.0)
    wupt = AR.alloc("wupt", [33, 1024], BF16)
    dma("pool", wupt.h[:, :], wup, [], [wupt], newsem("s_wup"))

    wkB = AR.alloc("wkB", [128, DC, 512], BF16)
    wvB = AR.alloc("wvB", [128, DC, 1024], BF16)
    wddB = AR.alloc("wddB", [128, DC, 32], BF16)
    wload(wddB, w_in[:, C_DD:C_DD + 32], 32, newsem("s_wddB"))
    wload(wkB, w_in[:, C_K:C_K + 512], 512, newsem("s_wkB"))
    wload(wvB, w_in[:, C_V:C_V + 1024], 1024, newsem("s_wvB"))
    wq = AR.alloc("wq", [128, DC, 512], BF16)
    wk = AR.alloc("wk1", [128, DC, 512], BF16)
    wg = AR.alloc("wg", [128, DC, 1024], BF16)
    wdd = AR.alloc("wdd1", [128, DC, 32], BF16)
    ghb = AR.alloc("ghb", [128, NH * HV], F32)

    RS = []
    for s in range(3):
        R = {
            "xt": AR.alloc(f"xt{s}", [128, DC, TT], F32), "xt_sem": newsem(f"s_xt{s}"),
            "hT": AR.alloc(f"hT{s}", [128, DC, TT], BF16), "hT_sem": newsem(f"s_hT{s}"),
            "ssb": AR.alloc(f"ssb{s}", [128, NH * HV], BF16), "ssb_sem": newsem(f"s_ssb{s}"), "v_sem": newsem(f"s_v{s}"),
            "sq": AR.alloc(f"sq{s}", [128, DC, TT], BF16), "rs": AR.alloc(f"rs{s}", [128, TT], F32), "tmp": AR.alloc(f"tmp{s}", [128, DC, TT], F32),
            "dda": AR.alloc(f"dda{s}", [33, TT], BF16),
            "nl": AR.alloc(f"nl{s}", [128, 512], F32), "cm": AR.alloc(f"cm{s}", [128, NH, TT], F32),
            "cmid": [AR.alloc(f"cmid{s}{d}", [128, NH], F32) for d in range(2)],
            "Eq": (lambda l: l * 2 if s == 2 else l + [AR.alloc(f"Eq{s}1", [128, NH, TT], F32)])([AR.alloc(f"Eq{s}0", [128, NH, TT], F32)]),
            "Ek": (lambda l: l * 2 if s == 2 else l + [AR.alloc(f"Ek{s}1", [128, NH, TT], F32)])([AR.alloc(f"Ek{s}0", [128, NH, TT], F32)]),
            "em": [AR.alloc(f"em{s}{d}", [128, NH], F32) for d in range(2)],
            "dd": [AR.alloc(f"dd{s}{d}", [128, NH], F32) for d in range(2)],
            "kvs": AR.alloc(f"kvs{s}", [128, NH, HV], F32),
            "ktT": (lambda l: l * 2 if s == 2 else l + [AR.alloc(f"ktT{s}1", [128, NH, TT], BF16)])([AR.alloc(f"ktT{s}0", [128, NH, TT], BF16)]),
            "ktm": AR.alloc(f"ktm{s}", [128, NH, HK], BF16),
            "v_sb": AR.alloc(f"v_sb{s}", [128, NH * HV], BF16),
        }
        op("dve", "memset", [], [R["dda"]], R["dda"].h[32:33, :], 1.0)
        RS.append(R)

    orderB = [("c", t, 0) for t in range(NCT)] + [("c", t, 1) for t in reversed(range(NCT))] + [("l", t, 1) for t in reversed(range(NT))]
    doneB = [False] * len(orderB)

    def genB(j, item, s):
        kind, t, d = item
        is_ctx = kind == "c"
        R = RS[s]
        bA, bB = 2 * s, 2 * s + 1
        xt, hT = R["xt"], R["hT"]
        dma("sp", xt.h[:, :, :], (ctx_t if is_ctx else x_t)[t], [], [xt], R["xt_sem"])
        yield
        yield from gen_hT(xt, hT, V_GS1C if is_ctx else V_GS1, V_SH1C if is_ctx else V_SH1, bA, R["sq"], R["rs"], R["tmp"])
        if not is_ctx:
            dma("sp", hT_s[t], hT.h[:, :, :], [hT], [hT_b[t]], R["hT_sem"])
        for c in range(DC):
            mm(psf(bB, TT)[0:32, :], wddB.h[:, c, :], hT.h[:, c, :], c == 0, c == DC - 1, [wddB, hT], [PS[bB]])
        op("act", "copy", [PS[bB]], [R["dda"]], out=R["dda"].h[0:32, :], in_=psf(bB, TT)[0:32, :])
        for h in range(NH):
            for c in range(DC):
                mm(psf(bA, NH, TT)[:, h, :], wkB.h[:, c, h * HK:(h + 1) * HK], hT.h[:, c, :], c == 0, c == DC - 1, [wkB, hT], [PS[bA]])
        yield
        yield from decay_dir(R, d, bB, bB, False)
        yield
        ktT = R["ktT"][d]
        op("dve", "tensor_tensor", [PS[bA], R["Ek"][d]], [ktT], out=ktT.h[:, :, :], in0=psf(bA, NH, TT), in1=R["Ek"][d].h[:, :, :], op=ALU.mult)
        yield
        v_sb = R["v_sb"]
        for blk, b in ((0, bB), (1, bA)):
            for c in range(DC):
                mm(psf(b, 512), hT.h[:, c, :], wvB.h[:, c, blk * 512:(blk + 1) * 512], c == 0, c == DC - 1, [hT, wvB], [PS[b]])
            op("act", "copy", [PS[b]], [v_sb], out=v_sb.h[:, blk * 512:(blk + 1) * 512], in_=psf(b, 512))
            yield
        if not is_ctx:
            dma("sp", v_s[t], v_sb.h[:, :], [v_sb], [v_b[t]], R["v_sem"])
        ktm = R["ktm"]
        for h in range(NH):
            op("pe", "transpose", [ktT, ident], [PS[bB]], psb(bB, NH, HK)[:, h, :], ktT.h[:, h, :], ident.h[:, :])
        op("dve", "tensor_copy", [PS[bB]], [ktm], out=ktm.h[:, :, :], in_=psb(bB, NH, HK))
        yield
        for h in range(NH):
            b = bA + h // 2
            mm(psf(b, 2, HV)[:, h % 2, :], ktm.h[:, h, :], v_sb.h[:, h * HV:(h + 1) * HV], True, True, [ktm, v_sb], [PS[b]])
        yield
        while j > 0 and not doneB[j - 1]:
            yield
        S = S_f if d == 0 else S_b
        if not is_ctx:
            sb_ = R["ssb"]
            op("pool", "tensor_tensor", [S, R["em"][d]], [sb_], out=sb_.h[:, :].rearrange("p (h v) -> p h v", h=NH), in0=S.h[:, :, :],
               in1=R["em"][d].h[:, :].unsqueeze(2).to_broadcast([128, NH, HV]), op=ALU.mult)
            dma("sp", ssb_s[t], sb_.h[:, :], [sb_], [ssb_b[t]], R["ssb_sem"])
        state_update(S, bA, R["Eq"][d], R["dd"][d], TT - 1 if d == 0 else 0, R["kvs"])
        doneB[j] = True
        yield

    s_w1 = {n: newsem("s_w1" + n) for n in ("q", "k", "v", "g", "dd", "ga", "go")}
    dma("sp", ghb.h[:, :], g_head.partition_broadcast(128), [], [ghb], newsem("s_ghb"))
    wload(wdd, w_in[:, C_DD:C_DD + 32], 32, s_w1["dd"])
    wload(wq, w_in[:, C_Q:C_Q + 512], 512, s_w1["q"])
    wload(wk, w_in[:, C_K:C_K + 512], 512, s_w1["k"])
    wload(wg, w_in[:, C_G:C_G + 1024], 1024, s_w1["g"])

    interleave(genB, orderB, 5, K=3)
    if dbg:
        dbg_t["sf"] = nc.dram_tensor("dbg_sf", [128, NH, HV], F32, kind="ExternalOutput").ap()
        dma("sp", dbg_t["sf"], S_f.h[:, :, :], [S_f], [], newsem("s_dbgsf"))
    AR.free(wkB, wvB, wddB, S_b)
    for R in RS:
        AR.free(R["xt"], R["sq"], R["rs"], R["tmp"])
    R = RS.pop()
    AR.free(R["hT"], R["ssb"], R["dda"], R["nl"], R["cm"], *R["cmid"], R["Eq"][0], R["Ek"][0], *R["em"], *R["dd"], R["kvs"], R["ktT"][0], R["ktm"], R["v_sb"])
    wga = AR.alloc("wga", [128, DC, 1024], BF16)
    wgo = AR.alloc("wgo", [128, DC, 1024], BF16)
    wload(wga, w_in[:, C_GA:C_GA + 1024], 1024, s_w1["ga"])
    wload(wgo, w_go, 1024, s_w1["go"])

    for s in range(2):
        R = RS[s]
        R["qtT"] = [AR.alloc(f"qtT{s}{d}", [128, NH, TT], BF16) for d in range(2)]
        R["scm"] = [AR.alloc(f"scm{s}{d}", [128, NH, TT], BF16) for d in range(2)]
        R["ssf"] = AR.alloc(f"ssf{s}", [128, NH, HV], BF16)
        R["sgg"] = AR.alloc(f"sgg{s}", [128, NH * HV], F32)
        R["ssq"] = AR.alloc(f"ssq{s}", [128, NH], F32)
        R["rso"] = AR.alloc(f"rso{s}", [128, NH], F32)
        R["og"] = AR.alloc(f"og{s}", [128, NH * HV], BF16)
        R["ogT"] = AR.alloc(f"ogT{s}", [128, DC, TT], BF16)
        R["th"] = AR.alloc(f"th{s}", [128, DC, TT], F32)
        R["ma"] = AR.alloc(f"ma{s}", [128, DC, TT], F32)
        R["ma_sem"] = newsem(f"s_ma{s}")
    doneF = [False] * NT

    op("dve", "tensor_scalar", [ghb], [ghb], out=ghb.h[:, :], in0=ghb.h[:, :], scalar1=0.5, scalar2=None, op0=ALU.mult)

    def decay_post(R, d, cb, cm):
        mid = 63 if d == 0 else 64
        last = TT - 1 if d == 0 else 0
        cmid, Eq, Ek, em, dd_ = R["cmid"][d], R["Eq"][d], R["Ek"][d], R["em"][d], R["dd"][d]
        op("dve", "tensor_copy", [PS[cb]], [cmid], out=cmid.h[:, :], in_=psf(cb, NH, TT)[:, :, mid])
        op("dve", "tensor_tensor", [PS[cb], cmid], [cm[1]], out=cm[0], in0=psf(cb, NH, TT), in1=cmid.h[:, :].unsqueeze(2).to_broadcast([128, NH, TT]), op=ALU.subtract)
        op("act", "activation", [cm[1]], [Eq], out=Eq.h[:, :, :], in_=cm[0], func=AF.Exp, scale=-1.0 / TAU)
        op("act", "activation", [cm[1]], [Ek], out=Ek.h[:, :, :], in_=cm[0], func=AF.Exp, scale=1.0 / TAU)
        op("act", "activation", [cmid], [em], out=em.h[:, :], in_=cmid.h[:, :], func=AF.Exp, scale=-1.0 / TAU)
        op("dve", "tensor_tensor", [em, Eq], [dd_], out=dd_.h[:, :], in0=em.h[:, :], in1=Eq.h[:, :, last], op=ALU.mult)

    def genF1(j, t, s):
        R = RS[s]
        b0, b1, b2, b3 = 4 * s, 4 * s + 1, 4 * s + 2, 4 * s + 3
        hT, dda, v_sb, ktm = R["hT"], R["dda"], R["v_sb"], R["ktm"]
        qtT, ktT, scm, sgg, th = R["qtT"], R["ktT"], R["scm"], R["sgg"], R["th"]
        nl = [(R["nl"].h[:, :], R["nl"]), (R["kvs"].h[:, 0:2, :].rearrange("p a b -> p (a b)"), R["kvs"])]
        cm = [(R["cm"].h[:, :, :], R["cm"]), (th.h[:, 0:4, :], th)]
        dma("sp", hT.h[:, :, :], hT_s[t], [hT_b[t]], [hT], R["hT_sem"])
        dma("sp", R["ssb"].h[:, :], ssb_s[t], [ssb_b[t]], [R["ssb"]], R["ssb_sem"])
        dma("sp", v_sb.h[:, :], v_s[t], [v_b[t]], [v_sb], R["v_sem"])
        yield
        for c in range(DC):
            mm(psf(b0, TT)[0:32, :], wdd.h[:, c, :], hT.h[:, c, :], c == 0, c == DC - 1, [wdd, hT], [PS[b0]])
        op("act", "copy", [PS[b0]], [dda], out=dda.h[0:32, :], in_=psf(b0, TT)[0:32, :])
        yield
        for d, zb in ((0, b1), (1, b2)):
            mm(psf(zb, 512), dda.h[:, :], wupt.h[:, d * 512:(d + 1) * 512], True, True, [dda, wupt], [PS[zb]])
            op("act", "activation", [PS[zb]], [nl[d][1]], out=nl[d][0], in_=psf(zb, 512), func=AF.Exp, scale=-1.0)
            op("act", "activation", [nl[d][1]], [nl[d][1]], out=nl[d][0], in_=nl[d][0], func=AF.Ln, bias=1.0)
        for (wt, b) in ((wq, b3), (wk, b0)):
            for h in range(NH):
                for c in range(DC):
                    mm(psf(b, NH, TT)[:, h, :], wt.h[:, c, h * HK:(h + 1) * HK], hT.h[:, c, :], c == 0, c == DC - 1, [wt, hT], [PS[b]])
            yield
        for d, cb in ((0, b1), (1, b2)):
            tri = trif if d == 0 else trib
            for h in range(NH):
                mm(psf(cb, NH, TT)[:, h, :], nl[d][0][:, h * HK:(h + 1) * HK], tri.h[:, :], True, True, [nl[d][1], tri], [PS[cb]])
            decay_post(R, d, cb, cm[d])
        yield
        for d in range(2):
            op("dve", "scalar_tensor_tensor", [PS[b3], R["Eq"][d]], [qtT[d]], out=qtT[d].h[:, :, :], in0=psf(b3, NH, TT), scalar=float(HK) ** -0.5,
               in1=R["Eq"][d].h[:, :, :], op0=ALU.mult, op1=ALU.mult)
            op("dve", "tensor_tensor", [PS[b0], R["Ek"][d]], [ktT[d]], out=ktT[d].h[:, :, :], in0=psf(b0, NH, TT), in1=R["Ek"][d].h[:, :, :], op=ALU.mult)
        yield
        for blk, b in ((0, b1), (1, b2)):
            for c in range(DC):
                mm(psf(b, 512), hT.h[:, c, :], wg.h[:, c, blk * 512:(blk + 1) * 512], c == 0, c == DC - 1, [hT, wg], [PS[b]])
            op("act", "activation", [PS[b]], [sgg], out=sgg.h[:, blk * 512:(blk + 1) * 512], in_=psf(b, 512), func=AF.Tanh, scale=0.5)
            op("dve", "scalar_tensor_tensor", [sgg, PS[b]], [sgg], out=sgg.h[:, blk * 512:(blk + 1) * 512], in0=sgg.h[:, blk * 512:(blk + 1) * 512], scalar=1.0,
               in1=psf(b, 512), op0=ALU.add, op1=ALU.mult)
        op("pool", "tensor_tensor", [sgg, ghb], [sgg], out=sgg.h[:, :], in0=sgg.h[:, :], in1=ghb.h[:, :], op=ALU.mult)
        yield
        for h in range(NH):
            op("pe", "transpose", [ktT[0], ident], [PS[b3]], psb(b3, NH, HK)[:, h, :], ktT[0].h[:, h, :], ident.h[:, :])
        op("dve", "tensor_copy", [PS[b3]], [ktm], out=ktm.h[:, :, :], in_=psb(b3, NH, HK))
        for h in range(NH):
            mm(psf(b0, NH, TT)[:, h, :], ktT[0].h[:, h, :], qtT[0].h[:, h, :], True, True, [ktT[0], qtT[0]], [PS[b0]])
        op("dve", "tensor_tensor", [PS[b0], trif], [scm[0]], out=scm[0].h[:, :, :], in0=psf(b0, NH, TT), in1=trif.h[:, :].unsqueeze(1).to_broadcast([128, NH, TT]), op=ALU.mult)
        yield
        for h in range(NH):
            mm(psf(b3, NH, TT)[:, h, :], ktT[1].h[:, h, :], qtT[1].h[:, h, :], True, True, [ktT[1], qtT[1]], [PS[b3]])
        op("dve", "tensor_tensor", [PS[b3], trib], [scm[1]], out=scm[1].h[:, :, :], in0=psf(b3, NH, TT), in1=trib.h[:, :].unsqueeze(1).to_broadcast([128, NH, TT]), op=ALU.mult)
        yield
        while j > 0 and not doneF[j - 1]:
            yield
        ssf = R["ssf"]
        op("pool", "tensor_tensor", [S_f, R["em"][0]], [ssf], out=ssf.h[:, :, :], in0=S_f.h[:, :, :], in1=R["em"][0].h[:, :].unsqueeze(2).to_broadcast([128, NH, HV]), op=ALU.mult)
        for h in range(NH):
            b = b1 + h // 2
            mm(psf(b, 2, HV)[:, h % 2, :], ktm.h[:, h, :], v_sb.h[:, h * HV:(h + 1) * HV], True, True, [ktm, v_sb], [PS[b]])
        state_update(S_f, b1, R["Eq"][0], R["dd"][0], TT - 1, R["kvs"])
        doneF[j] = True
        yield
        for h in range(NH):
            b = b0 if h < 2 else b3
            oap = psf(b, 2, HV)[:, h % 2, :]
            vh = v_sb.h[:, h * HV:(h + 1) * HV]
            mm(oap, scm[0].h[:, h, :], vh, True, False, [scm[0], v_sb], [PS[b]])
            mm(oap, scm[1].h[:, h, :], vh, False, False, [scm[1], v_sb], [PS[b]])
            mm(oap, qtT[0].h[:, h, :], ssf.h[:, h, :], False, False, [qtT[0], ssf], [PS[b]])
            mm(oap, qtT[1].h[:, h, :], R["ssb"].h[:, h * HV:(h + 1) * HV], False, True, [qtT[1], R["ssb"]], [PS[b]])
        osq, ssq, rso, og = R["kvs"], R["ssq"], R["rso"], R["og"]
        for hh, b in ((0, b0), (1, b3)):
            op("act", "activation", [PS[b]], [osq], out=osq.h[:, hh * 2:(hh + 1) * 2, :], in_=psf(b, 2, HV), func=AF.Square)
        yield
        for co in range(DC):
            b = b1 + co // 4
            for c in range(DC):
                mm(psf(b, 4, TT)[:, co % 4, :], wga.h[:, c, co * 128:(co + 1) * 128], hT.h[:, c, :], c == 0, c == DC - 1, [wga, hT], [PS[b]])
            if co % 4 == 3:
                hh = co // 4
                op("act", "activation", [PS[b]], [th], out=th.h[:, hh * 4:(hh + 1) * 4, :], in_=psf(b, 4, TT), func=AF.Tanh, scale=0.5)
        op("dve", "reduce_sum", [osq], [ssq], out=ssq.h[:, :], in_=osq.h[:, :, :], axis=mybir.AxisListType.X)
        op("act", "activation", [ssq], [rso], out=rso.h[:, :], in_=ssq.h[:, :], func=AF.Ln, scale=1.0 / HV, bias=EPS)
        op("act", "activation", [rso], [rso], out=rso.h[:, :], in_=rso.h[:, :], func=AF.Exp, scale=-0.5)
        yield
        for h in range(NH):
            b = b0 if h < 2 else b3
            op("dve", "scalar_tensor_tensor", [PS[b], rso, sgg], [og], out=og.h[:, h * HV:(h + 1) * HV], in0=psf(b, 2, HV)[:, h % 2, :], scalar=rso.h[:, h:h + 1],
               in1=sgg.h[:, h * HV:(h + 1) * HV], op0=ALU.mult, op1=ALU.mult)
        yield
        ogT = R["ogT"]
        for c in range(DC):
            op("pe", "transpose", [og, ident], [PS[b0]], psb(b0, DC, TT)[:, c, :], og.h[:, c * 128:(c + 1) * 128], ident.h[:, :])
        op("act", "copy", [PS[b0]], [ogT], out=ogT.h[:, :, :], in_=psb(b0, DC, TT))
        yield
        ma = R["ma"]
        for co in range(DC):
            b = b3 if co < 4 else b1
            for c in range(DC):
                mm(psf(b, 4, TT)[:, co % 4, :], wgo.h[:, c, co * 128:(co + 1) * 128], ogT.h[:, c, :], c == 0, c == DC - 1, [wgo, ogT], [PS[b]])
            if co % 4 == 3:
                hh = co // 4
                op("dve", "scalar_tensor_tensor", [th, PS[b]], [ma], out=ma.h[:, hh * 4:(hh + 1) * 4, :], in0=th.h[:, hh * 4:(hh + 1) * 4, :], scalar=1.0,
                   in1=psf(b, 4, TT), op0=ALU.add, op1=ALU.mult)
                yield
        dma("sp", ma_s[t], ma.h[:, :, :], [ma], [ma_b[t]], R["ma_sem"])
        yield

    interleave(genF1, list(range(NT)), 9)
    AR.free(wq, wk, wg, wdd, wga, wgo, ghb, S_f, wupt)
    for R in RS:
        AR.free(R["ssb"], R["dda"], R["nl"], R["cm"], *R["cmid"], *R["Eq"], *R["Ek"], *R["em"], *R["dd"], R["kvs"], *R["ktT"], R["ktm"], R["v_sb"],
                *R["qtT"], *R["scm"], R["ssf"], R["sgg"], R["ssq"], R["rso"], R["og"], R["ogT"], R["th"])

    wcb = AR.alloc("wcb", [128, DC, 1024], BF16)
    wcc = AR.alloc("wcc", [128, DC, 1024], BF16)
    wch = AR.alloc("wch", [128, DC, 1024], BF16)
    wgb = AR.alloc("wgb", [128, DC, 1024], BF16)
    wco = AR.alloc("wco", [128, DC, 1024], BF16)
    wmo = AR.alloc("wmo", [128, DC, 1024], BF16)
    wload(wcc, w_in[:, C_CC:C_CC + 1024], 1024, newsem("s_w2cc"))
    wload(wch, w_in[:, C_CH:C_CH + 1024], 1024, newsem("s_w2ch"))
    wload(wcb, w_in[:, C_CB:C_CB + 1024], 1024, newsem("s_w2cb"))
    wload(wgb, w_in[:, C_GB:C_GB + 1024], 1024, newsem("s_w2gb"))
    wload(wco, w_co, 1024, newsem("s_w2co"))
    wload(wmo, w_mo, 1024, newsem("s_w2mo"))
    for s in range(2):
        R = RS[s]
        R["xt"] = AR.alloc(f"xtb{s}", [128, DC, TT], F32)
        for n in ("cch", "u", "uc", "cbs", "thb", "mrg", "t1"):
            R[n] = AR.alloc(f"{n}{s}", [128, DC, TT], F32)
        R["x1"] = x1_pre[s]
        R["ycin"] = AR.alloc(f"ycin{s}", [128, DC, TT], BF16)
        R["mrgb"] = AR.alloc(f"mrgb{s}", [128, DC, TT], BF16)
        R["x1_sem"] = newsem(f"s_x1{s}")

    def feat_mm(wt, rhs_t, bpair, evac):
        for co in range(DC):
            b = bpair + co // 4
            for c in range(DC):
                mm(psf(b, 4, TT)[:, co % 4, :], wt.h[:, c, co * 128:(co + 1) * 128], rhs_t.h[:, c, :], c == 0, c == DC - 1, [wt, rhs_t], [PS[b]])
            if co % 4 == 3:
                evac(co // 4, b)

    def wcv_bc(k):
        return wcv.h[:, :, k:k + 1]

    def genF2(j, t, s):
        R = RS[s]
        b0 = 4 * s
        hT, xt, ma = R["hT"], R["xt"], R["ma"]
        cch, u, uc, cbs, thb, mrg, x1t, t1, ycin, mrgb = (R[n] for n in ("cch", "u", "uc", "cbs", "thb", "mrg", "x1", "t1", "ycin", "mrgb"))
        dma("sp", hT.h[:, :, :], hT_s[t], [hT_b[t]], [hT], R["hT_sem"])
        dma("sp", ma.h[:, :, :], ma_s[t], [ma_b[t]], [ma], R["ma_sem"])
        dma("sp", xt.h[:, :, :], x_t[t], [], [xt], R["xt_sem"])
        yield
        sl = lambda hh: slice(hh * 4, (hh + 1) * 4)
        feat_mm(wcc, hT, b0, lambda hh, b: op("act", "copy", [PS[b]], [cch], out=cch.h[:, sl(hh), :], in_=psf(b, 4, TT)))
        yield
        feat_mm(wch, hT, b0 + 2, lambda hh, b: op("dve", "tensor_tensor", [PS[b], cch], [u], out=u.h[:, sl(hh), :], in0=psf(b, 4, TT), in1=cch.h[:, sl(hh), :], op=ALU.mult))
        yield
        feat_mm(wcb, hT, b0, lambda hh, b: op("act", "copy", [PS[b]], [cbs], out=cbs.h[:, sl(hh), :], in_=psf(b, 4, TT)))
        yield
        feat_mm(wgb, hT, b0 + 2, lambda hh, b: op("act", "activation", [PS[b]], [thb], out=thb.h[:, sl(hh), :], in_=psf(b, 4, TT), func=AF.Tanh, scale=0.5))
        yield
        u4 = u.h[:, :, :].rearrange("p c (r w) -> p c r w", w=64)
        uc4 = uc.h[:, :, :].rearrange("p c (r w) -> p c r w", w=64)
        t14 = t1.h[:, :, :].rearrange("p c (r w) -> p c r w", w=64)
        op("dve", "tensor_tensor", [u, wcv], [uc], out=uc.h[:, :, :], in0=u.h[:, :, :], in1=wcv.h[:, :, 1:2].to_broadcast([128, DC, TT]), op=ALU.mult)
        op("pool", "tensor_tensor", [u, wcv], [t1], out=t14[:, :, :, 1:64], in0=u4[:, :, :, 0:63], in1=wcv.h[:, :, 0:1].unsqueeze(3).to_broadcast([128, DC, 2, 63]), op=ALU.mult)
        yield
        op("dve", "tensor_tensor", [uc, t1], [uc], out=uc4[:, :, :, 1:64], in0=uc4[:, :, :, 1:64], in1=t14[:, :, :, 1:64], op=ALU.add)
        op("pool", "tensor_tensor", [u, wcv, t1], [t1], out=t14[:, :, :, 0:63], in0=u4[:, :, :, 1:64], in1=wcv.h[:, :, 2:3].unsqueeze(3).to_broadcast([128, DC, 2, 63]), op=ALU.mult)
        yield
        op("dve", "tensor_tensor", [uc, t1], [uc], out=uc4[:, :, :, 0:63], in0=uc4[:, :, :, 0:63], in1=t14[:, :, :, 0:63], op=ALU.add)
        op("dve", "tensor_tensor", [cbs, uc], [ycin], out=ycin.h[:, :, :], in0=cbs.h[:, :, :], in1=uc.h[:, :, :], op=ALU.mult)
        yield

        def ev_co(hh, b):
            op("dve", "scalar_tensor_tensor", [thb, PS[b]], [mrg], out=mrg.h[:, sl(hh), :], in0=thb.h[:, sl(hh), :], scalar=1.0, in1=psf(b, 4, TT), op0=ALU.add, op1=ALU.mult)
            op("pool", "tensor_tensor", [mrg, ma], [mrg], out=mrg.h[:, sl(hh), :], in0=mrg.h[:, sl(hh), :], in1=ma.h[:, sl(hh), :], op=ALU.add)
            op("act", "activation", [mrg], [mrgb], out=mrgb.h[:, sl(hh), :], in_=mrg.h[:, sl(hh), :], func=AF.Identity, scale=0.5)
        feat_mm(wco, ycin, b0, ev_co)
        yield

        def ev_mo(hh, b):
            op("dve", "tensor_tensor", [PS[b], vec], [t1], out=t1.h[:, sl(hh), :], in0=psf(b, 4, TT), in1=vec.h[:, V_GT1, hh * 4:(hh + 1) * 4].unsqueeze(2).to_broadcast([128, 4, TT]), op=ALU.mult)
            op("pool", "tensor_tensor", [t1, xt], [x1t], out=x1t.h[:, sl(hh), :], in0=t1.h[:, sl(hh), :], in1=xt.h[:, sl(hh), :], op=ALU.add)
        feat_mm(wmo, mrgb, b0 + 2, ev_mo)
        yield
        dma("sp", x1_s[t], x1t.h[:, :, :], [x1t], [x1_b[t]], R["x1_sem"])
        yield

    interleave(genF2, list(range(NT)), 6)
    AR.free(wcb, wcc, wch, wgb, wco, wmo, wcv)
    for R in RS:
        AR.free(R["hT"], R["ma"], R["xt"], R["cch"], R["u"], R["uc"], R["cbs"], R["thb"], R["mrg"], R["t1"], R["ycin"], R["mrgb"])

    NG = (FC + 3) // 4
    wfi_g, wfo_g = [], []
    wst = AR.alloc("wst", [128, DC, 512], F32)
    s_wst = newsem("s_wst")
    late_w = {}
    for g in range(NG):
        nf = min(4, FC - g * 4)
        wi = AR.alloc(f"wfi{g}", [128, DC, 2 * nf * 128], BF16)
        wo = AR.alloc(f"wfo{g}", [128, nf, D], BF16)
        blocks = [(wi.h[:, :, 0:nf * 128], wi, wst.h[:, :, 0:nf * 128], w_fi[:, g * 512:g * 512 + nf * 128].rearrange("(c p) n -> p c n", p=128)),
                  (wi.h[:, :, nf * 128:2 * nf * 128], wi, wst.h[:, :, 0:nf * 128], w_fi[:, DFF + g * 512:DFF + g * 512 + nf * 128].rearrange("(c p) n -> p c n", p=128))]
        for hh in range(2):
            blocks.append((wo.h[:, :, hh * 512:(hh + 1) * 512], wo, wst.h[:, 0:nf, :],
                           w_fo[g * 512:g * 512 + nf * 128, hh * 512:(hh + 1) * 512].rearrange("(c p) n -> p c n", p=128)))
        if g % 2 == 0:
            si, so = newsem(f"s_wfi{g}"), newsem(f"s_wfo{g}")
            for (dst, dt_, _st, src_) in blocks:
                dma("pool", dst, src_, [], [dt_], si if dt_ is wi else so)
        else:
            late_w[g] = blocks
        wfi_g.append(wi)
        wfo_g.append(wo)
    for s in range(2):
        R = RS[s]
        R["sq"] = AR.alloc(f"sq3{s}", [128, DC, TT], BF16)
        R["rs"] = AR.alloc(f"rs3{s}", [128, TT], F32)
        R["tmp"] = AR.alloc(f"tmp3{s}", [128, DC, TT], F32)
        R["h2"] = AR.alloc(f"h2{s}", [128, DC, TT], BF16)
        R["sa"] = AR.alloc(f"sa{s}", [128, 4, TT], F32)
        R["hff"] = [AR.alloc(f"hff{s}{i}", [128, 4, TT], BF16) for i in range(2)]
        R["x2"] = AR.alloc(f"x2{s}", [128, DC, TT], F32)
        R["sqb"] = AR.alloc(f"sqb{s}", [128, DC, TT], BF16)
        R["rsb"] = AR.alloc(f"rsb{s}", [128, TT], F32)
        R["ot_sem"] = newsem(f"s_ot{s}")
    ot_sems = [RS[0]["ot_sem"], RS[1]["ot_sem"]]

    def genF3(j, t, s):
        R = RS[s]
        b0 = 4 * s
        x1t, h2, x2, sa = R["x1"], R["h2"], R["x2"], R["sa"]
        dma("sp", x1t.h[:, :, :], x1_s[t], [x1_b[t]], [x1t], R["x1_sem"])
        yield
        yield from gen_hT(x1t, h2, V_GS2, V_SH2, b0, R["sq"], R["rs"], R["tmp"])
        def out_proj(g):
            nf = min(4, FC - g * 4)
            wo, hf = wfo_g[g], R["hff"][g % 2]
            for ff in range(nf):
                f = g * 4 + ff
                for co in range(DC):
                    b = b0 + 2 + co // 4
                    op("pe", "matmul", [wo, hf], [PS[b]], psf(b, 4, TT)[:, co % 4, :], lhsT=wo.h[:, ff, co * 128:(co + 1) * 128], rhs=hf.h[:, ff, :],
                       start=(f == 0 and co % 4 == 0), stop=(f == FC - 1), skip_group_check=True)

        for g in range(NG):
            nf = min(4, FC - g * 4)
            wi, hf = wfi_g[g], R["hff"][g % 2]
            if j == 0 and g in late_w:
                for (dst, dt_, st_, src_) in late_w[g]:
                    dma("sp", st_, src_, [], [wst], s_wst)
                    op("dve", "tensor_copy", [wst], [dt_], out=dst, in_=st_)
            for (b, base) in ((b0, 0), (b0 + 1, nf * 128)):
                for ff in range(nf):
                    for c in range(DC):
                        mm(psf(b, 4, TT)[:, ff, :], wi.h[:, c, base + ff * 128: base + (ff + 1) * 128], h2.h[:, c, :], c == 0, c == DC - 1, [wi, h2], [PS[b]])
            op("act", "activation", [PS[b0]], [sa], out=sa.h[:, 0:nf, :], in_=psf(b0, 4, TT)[:, 0:nf, :], func=AF.Silu)
            op("dve", "tensor_tensor", [PS[b0 + 1], sa], [hf], out=hf.h[:, 0:nf, :], in0=psf(b0 + 1, 4, TT)[:, 0:nf, :], in1=sa.h[:, 0:nf, :], op=ALU.mult)
            yield
            if g > 0:
                out_proj(g - 1)
                yield
        out_proj(NG - 1)
        yield
        for hh in range(2):
            b = b0 + 2 + hh
            op("dve", "tensor_tensor", [PS[b], vec], [x2], out=x2.h[:, hh * 4:(hh + 1) * 4, :], in0=psf(b, 4, TT), in1=vec.h[:, V_GT2, hh * 4:(hh + 1) * 4].unsqueeze(2).to_broadcast([128, 4, TT]), op=ALU.mult)
        op("pool", "tensor_tensor", [x2, x1t], [x2], out=x2.h[:, :, :], in0=x2.h[:, :, :], in1=x1t.h[:, :, :], op=ALU.add)
        yield
        sq, rs = R["sqb"], R["rsb"]
        op("act", "activation", [x2], [sq], out=sq.h[:, :, :], in_=x2.h[:, :, :], func=AF.Square)
        yield
        for c in range(DC):
            mm(psf(b0, TT), ones.h[:, :], sq.h[:, c, :], c == 0, c == DC - 1, [ones, sq], [PS[b0]])
        yield
        op("act", "activation", [PS[b0]], [rs], out=rs.h[:, :], in_=psf(b0, TT), func=AF.Ln, scale=1.0 / D, bias=EPS)
        op("act", "activation", [rs], [rs], out=rs.h[:, :], in_=rs.h[:, :], func=AF.Exp, scale=-0.5)
        o_ = R["tmp"]
        op("pool", "tensor_tensor", [x2, rs], [o_], out=o_.h[:, :, :], in0=x2.h[:, :, :], in1=rs.h[:, :].unsqueeze(1).to_broadcast([128, DC, TT]), op=ALU.mult)
        op("dve", "tensor_tensor", [o_, vec], [o_], out=o_.h[:, :, :], in0=o_.h[:, :, :], in1=bc_c(vslot(V_GFIN), TT), op=ALU.mult)
        dma("sp", out_t[t], o_.h[:, :, :], [o_], [], R["ot_sem"])
        yield

    interleave(genF3, list(range(NT)), 10)

    def final_waits():
        return [(s_, 16 * P.dma_counts[id(s_)]) for s_ in ot_sems if id(s_) in P.dma_counts]

    P.emit(esems, final_waits)
    return nc, P


def _tile_fm(a):
    T = a.shape[0]
    return np.ascontiguousarray(a.reshape(T // TT, TT, DC, 128).transpose(0, 3, 2, 1))


def _untile_fm(a):
    nt = a.shape[0]
    return np.ascontiguousarray(a.transpose(0, 3, 2, 1).reshape(nt * TT, D))


def _fm(v):
    return np.ascontiguousarray(v.reshape(DC, 128).T)


def make_in_maps(x, c, ctx, c_ctx, w_ada, b_ada, g_mix, w_in, w_dec_up, b_dec, g_head, w_gla_out,
                 w_conv, w_conv_out, w_mix_out, g_ffn, w_ffn_in, w_ffn_out, g_final):
    f = lambda a: np.ascontiguousarray(np.asarray(a, dtype=np.float32))
    x, c, ctx, c_ctx = f(x), f(c), f(ctx), f(c_ctx)
    wup = np.zeros((33, 1024), np.float32)
    wup[0:16, 0:512] = f(w_dec_up)[0, 0]
    wup[16:32, 512:1024] = f(w_dec_up)[0, 1]
    wup[32, 0:512] = f(b_dec)[0, 0]
    wup[32, 512:1024] = f(b_dec)[0, 1]
    tri = np.triu(np.ones((128, 128), np.float32))
    shared = {
        "w_ada": f(w_ada)[0], "b_ada_fm": np.ascontiguousarray(f(b_ada)[0].reshape(48, 128).T),
        "g_mix_fm": _fm(f(g_mix)[0]), "g_ffn_fm": _fm(f(g_ffn)[0]), "g_fin_fm": _fm(f(g_final)),
        "w_in": f(w_in)[0], "wup_aug": wup, "g_head4": np.ascontiguousarray(np.tile(f(g_head)[0], NH)[None, :]),
        "w_gla_out": f(w_gla_out)[0], "w_conv_out": f(w_conv_out)[0], "w_mix_out": f(w_mix_out)[0],
        "w_conv_fm": np.ascontiguousarray(f(w_conv)[0].reshape(3, DC, 128).transpose(2, 1, 0)),
        "w_ffn_in": f(w_ffn_in)[0], "w_ffn_out": f(w_ffn_out)[0],
        "c_ident": np.eye(128, dtype=np.float32), "c_trif": tri, "c_trib": np.ascontiguousarray(tri.T),
    }
    maps = []
    for b in range(x.shape[0]):
        m = dict(shared)
        m["x_t"] = _tile_fm(x[b])
        m["ctx_t"] = _tile_fm(ctx[b])
        m["cvec"] = np.ascontiguousarray(np.stack([_fm(c[b]), _fm(c_ctx)], axis=-1))
        maps.append(m)
    return maps


_CACHE = {}


def kernel(**inputs):
    x = np.asarray(inputs["x"])
    B, T, _ = x.shape
    NT = T // TT
    NCT = np.asarray(inputs["ctx"]).shape[1] // TT
    key = (NT, NCT)
    if key not in _CACHE:
        _CACHE[key] = build_program(NT, NCT)[0]
    nc = _CACHE[key]
    in_maps = make_in_maps(**inputs)
    res = run_bass_kernel_spmd(nc, in_maps, core_ids=list(range(B)))
    out = np.stack([_untile_fm(np.asarray(r["out_t"])) for r in res.results], axis=0)
    return out.astype(np.float32)
```

```python
import numpy as np
import concourse.bass as bass
import concourse.mybir as mybir
from concourse.bass_utils import run_bass_kernel_spmd

F32 = mybir.dt.float32
BF16 = mybir.dt.bfloat16
AF = mybir.ActivationFunctionType
ALU = mybir.AluOpType

D = 1024
DC = 8
TT = 128
NH = 4
HK = 128
HV = 256
NIN = 8224
DFF = 2816
FC = 22
EPS = 1e-6
TAU = 16.0
C_Q, C_K, C_V, C_G, C_DD, C_CB, C_CC, C_CH, C_GA, C_GB = 0, 512, 1024, 2048, 3072, 3104, 4128, 5152, 6176, 7200

ENGS = ("pe", "act", "dve", "pool", "sp")


class Buf:
    __slots__ = ("name", "last_w", "readers")

    def __init__(self, name):
        self.name = name
        self.last_w = None
        self.readers = []


class Op:
    __slots__ = ("eng", "fn", "deps", "is_dma", "dsem", "dma_val", "needs_inc", "sem_val", "idx")


class Prog:
    def __init__(self, nc):
        self.nc = nc
        self.ops = []
        self.dma_counts = {}

    def add(self, eng, fn, reads=(), writes=(), dsem=None):
        op = Op()
        op.eng = eng
        op.fn = fn
        op.is_dma = dsem is not None
        op.dsem = dsem
        op.needs_inc = False
        op.sem_val = None
        op.idx = len(self.ops)
        op.dma_val = 0
        if op.is_dma:
            k = id(dsem)
            self.dma_counts[k] = self.dma_counts.get(k, 0) + 1
            op.dma_val = 16 * self.dma_counts[k]
        deps = {}
        wset = set(id(b) for b in writes)
        for b in reads:
            if b.last_w is not None:
                deps[b.last_w.idx] = b.last_w
        for b in writes:
            if b.last_w is not None:
                deps[b.last_w.idx] = b.last_w
            for r in b.readers:
                deps[r.idx] = r
        fdeps = []
        for d in deps.values():
            if d is op:
                continue
            if (not d.is_dma) and d.eng == "pe" and eng == "pe" and not op.is_dma:
                continue
            if not d.is_dma:
                d.needs_inc = True
            fdeps.append(d)
        op.deps = fdeps
        for b in writes:
            b.last_w = op
            b.readers = []
        for b in reads:
            if id(b) not in wset:
                if not op.is_dma:
                    b.readers = [r for r in b.readers if r.is_dma or r.eng != eng]
                b.readers.append(op)
        self.ops.append(op)
        return op

    def emit(self, sems, final_waits):
        nc = self.nc
        cnt = {e: 0 for e in ENGS}
        for op in self.ops:
            if not op.is_dma and op.needs_inc:
                cnt[op.eng] += 1
                op.sem_val = cnt[op.eng]
        with nc.Block() as block:
            for e in ENGS:
                eops = [o for o in self.ops if o.eng == e]

                def body(eng_handle, eops=eops, e=e):
                    waited = {}
                    for op in eops:
                        need = {}
                        for d in op.deps:
                            if d.is_dma:
                                s, v = d.dsem, d.dma_val
                            else:
                                s, v = sems[d.eng], d.sem_val
                            k = id(s)
                            if k not in need or need[k][1] < v:
                                need[k] = (s, v)
                        for k, (s, v) in need.items():
                            if waited.get(k, 0) >= v:
                                continue
                            eng_handle.wait_ge(s, v)
                            waited[k] = v
                        ins = op.fn(eng_handle)
                        if op.is_dma:
                            ins.then_inc(op.dsem, 16)
                        elif op.needs_inc:
                            ins.then_inc(sems[op.eng], 1)
                    if e == "sp":
                        for s, v in final_waits():
                            eng_handle.wait_ge(s, v)

                {"pe": block.tensor, "act": block.scalar, "dve": block.vector,
                 "pool": block.gpsimd, "sp": block.sync}[e](body)


class Tile:
    __slots__ = ("h", "buf", "off", "nbytes", "name")


class Arena:
    BASE = 16512
    END = 229344

    def __init__(self, nc):
        self.nc = nc
        self.live = []
        self.tomb = []
        self.n = 0

    def alloc(self, name, shape, dtype):
        esz = 4 if dtype == F32 else 2
        nbytes = int(np.prod(shape[1:])) * esz
        nbytes = (nbytes + 63) // 64 * 64
        occ = sorted((o, e) for o, e, _ in self.live)
        cur = self.BASE
        off = None
        for o, e in occ:
            if o - cur >= nbytes:
                off = cur
                break
            cur = max(cur, e)
        if off is None:
            if self.END - cur >= nbytes:
                off = cur
            else:
                raise RuntimeError(f"SBUF arena full allocating {name} {shape}")
        t = Tile()
        self.n += 1
        t.h = self.nc.alloc_sbuf_tensor_at(f"{name}_{self.n}", list(shape), dtype, offset=off)
        t.buf = Buf(name)
        t.off, t.nbytes, t.name = off, nbytes, name
        end = off + nbytes
        keep = []
        for (o, e, ops) in self.tomb:
            if o < end and off < e:
                t.buf.readers.extend(ops)
                if o >= off and e <= end:
                    continue
            keep.append((o, e, ops))
        self.tomb = keep
        self.live.append((off, end, t))
        return t

    def free(self, *tiles):
        for t in tiles:
            self.live = [x for x in self.live if x[2] is not t]
            ops = list(t.buf.readers)
            if t.buf.last_w is not None:
                ops.append(t.buf.last_w)
            if ops:
                self.tomb.append((t.off, t.off + t.nbytes, ops))


def build_program(NT=32, NCT=2, dbg=False):
    nc = bass.Bass("TRN2", target_bir_lowering=False)
    P = Prog(nc)
    AR = Arena(nc)
    sem_list = []

    def newsem(name):
        s = nc.alloc_semaphore(name)
        sem_list.append(s)
        return s

    esems = {e: newsem("e_" + e) for e in ("pe", "act", "dve", "pool")}

    def din(name, shape, dt=F32):
        return nc.dram_tensor(name, list(shape), dt, kind="ExternalInput").ap()

    def dscr(name, shape, dt):
        kind = "ExternalOutput" if dbg else "Internal"
        return nc.dram_tensor(name, list(shape), dt, kind=kind).ap()

    x_t = din("x_t", [NT, 128, DC, TT])
    ctx_t = din("ctx_t", [NCT, 128, DC, TT])
    cvec = din("cvec", [128, DC, 2])
    w_ada = din("w_ada", [D, 6 * D])
    b_ada = din("b_ada_fm", [128, 48])
    g_mix = din("g_mix_fm", [128, DC])
    g_ffn = din("g_ffn_fm", [128, DC])
    g_fin = din("g_fin_fm", [128, DC])
    w_in = din("w_in", [D, NIN])
    wup = din("wup_aug", [33, 1024])
    g_head = din("g_head4", [1, 1024])
    w_go = din("w_gla_out", [D, D])
    w_co = din("w_conv_out", [D, D])
    w_mo = din("w_mix_out", [D, D])
    w_cv = din("w_conv_fm", [128, DC, 3])
    w_fi = din("w_ffn_in", [D, 2 * DFF])
    w_fo = din("w_ffn_out", [DFF, D])
    c_id = din("c_ident", [128, 128])
    c_trf = din("c_trif", [128, 128])
    c_trb = din("c_trib", [128, 128])
    out_t = nc.dram_tensor("out_t", [NT, 128, DC, TT], F32, kind="ExternalOutput").ap()
    hT_s = dscr("hT_s", [NT, 128, DC, TT], BF16)
    ma_s = dscr("ma_s", [NT, 128, DC, TT], F32)
    x1_s = dscr("x1_s", [NT, 128, DC, TT], F32)
    ssb_s = dscr("ssb_s", [NT, 128, NH * HV], BF16)
    v_s = dscr("v_s", [NT, 128, NH * HV], BF16)
    v_b = [Buf(f"v_s{t}") for t in range(NT)]
    ssb_b = [Buf(f"ssb_s{t}") for t in range(NT)]
    dbg_t = {}

    hT_b = [Buf(f"hT_s{t}") for t in range(NT)]
    ma_b = [Buf(f"ma_s{t}") for t in range(NT)]
    x1_b = [Buf(f"x1_s{t}") for t in range(NT)]
    out_sem = newsem("s_out")

    PS = []
    for i in range(8):
        t = Tile()
        t.h = nc.alloc_psum_tensor(f"ps{i}", [128, 512], F32)
        t.buf = Buf(f"ps{i}")
        PS.append(t)

    def psf(i, *shape):
        ap = PS[i].h[:, :]
        n = int(np.prod(shape))
        ap = PS[i].h[:, 0:n]
        if len(shape) == 2:
            ap = ap.rearrange("p (a b) -> p a b", a=shape[0])
        return ap

    def psb(i, *shape):
        n = int(np.prod(shape))
        ap = PS[i].h[:, 0:n // 2].bitcast(BF16)
        if len(shape) == 2:
            ap = ap.rearrange("p (a b) -> p a b", a=shape[0])
        return ap

    def op(eng, meth, reads, writes, *a, **kw):
        return P.add(eng, lambda e: getattr(e, meth)(*a, **kw), [r.buf if isinstance(r, Tile) else r for r in reads],
                     [w.buf if isinstance(w, Tile) else w for w in writes])

    def dma(eng, out, in_, reads, writes, dsem):
        return P.add(eng, lambda e: e.dma_start(out=out, in_=in_), [r.buf if isinstance(r, Tile) else r for r in reads],
                     [w.buf if isinstance(w, Tile) else w for w in writes], dsem=dsem)

    def mm(out, lhsT, rhs, start, stop, reads, writes):
        return op("pe", "matmul", reads, writes, out, lhsT=lhsT, rhs=rhs, start=start, stop=stop)

    def wload(tile, col0, ncols, src, sem, rows_chunks=DC):
        pass

    ident = AR.alloc("ident", [128, 128], BF16)
    trif = AR.alloc("trif", [128, 128], F32)
    trib = AR.alloc("trib", [128, 128], F32)
    ones = AR.alloc("ones", [128, 128], BF16)
    s_c = newsem("s_const")
    dma("pool", ident.h[:, :], c_id, [], [ident], s_c)
    dma("sp", trif.h[:, :], c_trf, [], [trif], newsem("s_trf"))
    dma("sp", trib.h[:, :], c_trb, [], [trib], newsem("s_trb"))
    op("dve", "memset", [], [ones], ones.h[:, :], 1.0)

    x1_pre = [AR.alloc(f"x1_{s}", [128, DC, TT], F32) for s in range(2)]
    vec = AR.alloc("vec", [128, 16, DC], F32)
    V_GMIX, V_GFFN, V_GFIN, V_GS1, V_SH1, V_GS1C, V_SH1C, V_GT1, V_GS2, V_SH2, V_GT2 = range(11)
    s_vec = newsem("s_vec")
    dma("sp", vec.h[:, V_GMIX, :], g_mix, [], [vec], s_vec)
    dma("sp", vec.h[:, V_GFFN, :], g_ffn, [], [vec], s_vec)
    dma("sp", vec.h[:, V_GFIN, :], g_fin, [], [vec], s_vec)
    bada = AR.alloc("bada", [128, 48], F32)
    dma("sp", bada.h[:, :], b_ada, [], [bada], newsem("s_bada"))
    wcv = AR.alloc("wcv", [128, DC, 3], F32)
    dma("sp", wcv.h[:, :, :], w_cv, [], [wcv], newsem("s_wcv"))

    cv = AR.alloc("cv", [128, DC, 2], F32)
    cvb = AR.alloc("cvb", [128, DC, 2], BF16)
    dma("sp", cv.h[:, :, :], cvec, [], [cv], newsem("s_cv"))
    op("act", "activation", [cv], [cvb], out=cvb.h[:, :, :], in_=cv.h[:, :, :], func=AF.Silu)
    wa = [AR.alloc(f"wa{i}", [128, DC, 512], BF16) for i in range(4)]
    wa_s = [newsem(f"s_wa{i}") for i in range(4)]
    waf = [AR.alloc(f"waf{i}", [128, DC, 512], F32) for i in range(2)]
    waf_s = [newsem(f"s_waf{i}") for i in range(2)]
    modp = PS[0]
    for blk in range(12):
        w = wa[blk % 4]
        src_ = w_ada[:, blk * 512:(blk + 1) * 512].rearrange("(c p) n -> p c n", p=128)
        if blk % 2 == 0:
            dma("pool", w.h[:, :, :], src_, [], [w], wa_s[blk % 4])
        else:
            wf = waf[(blk // 2) % 2]
            dma("sp", wf.h[:, :, :], src_, [], [wf], waf_s[(blk // 2) % 2])
            op("dve", "tensor_copy", [wf], [w], out=w.h[:, :, :], in_=wf.h[:, :, :])
        for jj in range(4):
            j = blk * 4 + jj
            for c in range(DC):
                mm(psf(0, 48, 2)[:, j, :], w.h[:, c, jj * 128:(jj + 1) * 128], cvb.h[:, c, :], c == 0, c == DC - 1, [w, cvb], [modp])
    mod = AR.alloc("mod", [128, 48, 2], F32)
    op("dve", "tensor_tensor", [modp, bada], [mod], out=mod.h[:, :, :], in0=psf(0, 48, 2), in1=bada.h[:, :].unsqueeze(2).to_broadcast([128, 48, 2]), op=ALU.add)
    AR.free(*wa, *waf, cv, cvb, bada)

    def modv(k, s):
        return mod.h[:, k * 8:(k + 1) * 8, s]

    def vslot(i):
        return vec.h[:, i, :]

    op("dve", "scalar_tensor_tensor", [mod, vec], [vec], out=vslot(V_GS1), in0=modv(1, 0), scalar=1.0, in1=vslot(V_GMIX), op0=ALU.add, op1=ALU.mult)
    op("dve", "scalar_tensor_tensor", [mod, vec], [vec], out=vslot(V_GS1C), in0=modv(1, 1), scalar=1.0, in1=vslot(V_GMIX), op0=ALU.add, op1=ALU.mult)
    op("dve", "scalar_tensor_tensor", [mod, vec], [vec], out=vslot(V_GS2), in0=modv(4, 0), scalar=1.0, in1=vslot(V_GFFN), op0=ALU.add, op1=ALU.mult)
    op("dve", "tensor_copy", [mod], [vec], out=vslot(V_SH1), in_=modv(0, 0))
    op("dve", "tensor_copy", [mod], [vec], out=vslot(V_SH1C), in_=modv(0, 1))
    op("dve", "tensor_copy", [mod], [vec], out=vslot(V_GT1), in_=modv(2, 0))
    op("dve", "tensor_copy", [mod], [vec], out=vslot(V_SH2), in_=modv(3, 0))
    op("dve", "tensor_copy", [mod], [vec], out=vslot(V_GT2), in_=modv(5, 0))
    if dbg:
        dbg_t["mod"] = nc.dram_tensor("dbg_mod", [128, 48, 2], F32, kind="ExternalOutput").ap()
        dma("sp", dbg_t["mod"], mod.h[:, :, :], [mod], [], newsem("s_dbgmod"))

    def bc_c(ap2, n):
        return ap2.unsqueeze(2).to_broadcast([128, DC, n])

    def interleave(make_gen, order, offset, K=2):
        active = []
        j = 0
        while j < len(order) or active:
            if j < len(order) and all(a[0] != j % K for a in active) and (not active or active[-1][2] >= offset):
                active.append([j % K, make_gen(j, order[j], j % K), 0])
                j += 1
            for a in list(active):
                try:
                    next(a[1])
                    a[2] += 1
                except StopIteration:
                    active.remove(a)

    def gen_hT(xt, hT, gs_slot, sh_slot, bank, sq, rs, tmp):
        op("act", "activation", [xt], [sq], out=sq.h[:, :, :], in_=xt.h[:, :, :], func=AF.Square)
        yield
        for c in range(DC):
            mm(psf(bank, TT), ones.h[:, :], sq.h[:, c, :], c == 0, c == DC - 1, [ones, sq], [PS[bank]])
        yield
        op("act", "activation", [PS[bank]], [rs], out=rs.h[:, :], in_=psf(bank, TT), func=AF.Ln, scale=1.0 / D, bias=EPS)
        op("act", "activation", [rs], [rs], out=rs.h[:, :], in_=rs.h[:, :], func=AF.Exp, scale=-0.5)
        op("dve", "tensor_tensor", [xt, rs], [tmp], out=tmp.h[:, :, :], in0=xt.h[:, :, :], in1=rs.h[:, :].unsqueeze(1).to_broadcast([128, DC, TT]), op=ALU.mult)
        yield
        op("dve", "tensor_tensor", [tmp, vec], [tmp], out=tmp.h[:, :, :], in0=tmp.h[:, :, :], in1=bc_c(vslot(gs_slot), TT), op=ALU.mult)
        op("dve", "tensor_tensor", [tmp, vec], [hT], out=hT.h[:, :, :], in0=tmp.h[:, :, :], in1=bc_c(vslot(sh_slot), TT), op=ALU.add)
        yield

    def decay_dir(R, d, zb, cb, full_q):
        dda, nl, cm = R["dda"], R["nl"], R["cm"]
        mm(psf(zb, 512), dda.h[:, :], wupt.h[:, d * 512:(d + 1) * 512], True, True, [dda, wupt], [PS[zb]])
        op("act", "activation", [PS[zb]], [nl], out=nl.h[:, :], in_=psf(zb, 512), func=AF.Exp, scale=-1.0)
        op("act", "activation", [nl], [nl], out=nl.h[:, :], in_=nl.h[:, :], func=AF.Ln, bias=1.0)
        yield
        tri = trif if d == 0 else trib
        for h in range(NH):
            mm(psf(cb, NH, TT)[:, h, :], nl.h[:, h * HK:(h + 1) * HK], tri.h[:, :], True, True, [nl, tri], [PS[cb]])
        mid = 63 if d == 0 else 64
        last = TT - 1 if d == 0 else 0
        cmid, Eq, Ek, em, dd_ = R["cmid"][d], R["Eq"][d], R["Ek"][d], R["em"][d], R["dd"][d]
        op("dve", "tensor_copy", [PS[cb]], [cmid], out=cmid.h[:, :], in_=psf(cb, NH, TT)[:, :, mid])
        op("dve", "tensor_tensor", [PS[cb], cmid], [cm], out=cm.h[:, :, :], in0=psf(cb, NH, TT), in1=cmid.h[:, :].unsqueeze(2).to_broadcast([128, NH, TT]), op=ALU.subtract)
        if full_q:
            op("act", "activation", [cm], [Eq], out=Eq.h[:, :, :], in_=cm.h[:, :, :], func=AF.Exp, scale=-1.0 / TAU)
        else:
            op("act", "activation", [cm], [Eq], out=Eq.h[:, :, last:last + 1], in_=cm.h[:, :, last:last + 1], func=AF.Exp, scale=-1.0 / TAU)
        op("act", "activation", [cm], [Ek], out=Ek.h[:, :, :], in_=cm.h[:, :, :], func=AF.Exp, scale=1.0 / TAU)
        op("act", "activation", [cmid], [em], out=em.h[:, :], in_=cmid.h[:, :], func=AF.Exp, scale=-1.0 / TAU)
        op("dve", "tensor_tensor", [em, Eq], [dd_], out=dd_.h[:, :], in0=em.h[:, :], in1=Eq.h[:, :, last], op=ALU.mult)

    def state_update(S, kvb0, Eq, dd_, last, kvs):
        for h in range(NH):
            b = kvb0 + h // 2
            op("act", "activation", [PS[b], Eq], [kvs], out=kvs.h[:, h, :], in_=psf(b, 2, HV)[:, h % 2, :], func=AF.Identity, scale=Eq.h[:, h, last:last + 1])
        for h in range(NH):
            op("dve", "scalar_tensor_tensor", [S, dd_, kvs], [S], out=S.h[:, h, :], in0=S.h[:, h, :], scalar=dd_.h[:, h:h + 1], in1=kvs.h[:, h, :], op0=ALU.mult, op1=ALU.add)

    def wsrc(c0, n):
        return w_in[:, c0:c0 + n].rearrange("(c p) n -> p c n", p=128)

    def wload(wt, src2d, ncols, sem_):
        for c0 in range(0, ncols, 512):
            n = min(512, ncols - c0)
            dma("pool", wt.h[:, :, c0:c0 + n], src2d[:, c0:c0 + n].rearrange("(c p) n -> p c n", p=128), [], [wt], sem_)

    S_f = AR.alloc("S_f", [128, NH, HV], F32)
    S_b = AR.alloc("S_b", [128, NH, HV], F32)
    op("dve", "memset", [], [S_f], S_f.h[:, :, :], 0.0)
    op("pool", "memset", [], [S_b], S_b.h[:, :, :], 0.0)
    wupt = AR.alloc("wupt", [33, 1024], BF16)
    dma("pool", wupt.h[:, :], wup, [], [wupt], newsem("s_wup"))

    wkB = AR.alloc("wkB", [128, DC, 512], BF16)
    wvB = AR.alloc("wvB", [128, DC, 1024], BF16)
    wddB = AR.alloc("wddB", [128, DC, 32], BF16)
    wload(wddB, w_in[:, C_DD:C_DD + 32], 32, newsem("s_wddB"))
    wload(wkB, w_in[:, C_K:C_K + 512], 512, newsem("s_wkB"))
    wload(wvB, w_in[:, C_V:C_V + 1024], 1024, newsem("s_wvB"))
    wq = AR.alloc("wq", [128, DC, 512], BF16)
    wk = AR.alloc("wk1", [128, DC, 512], BF16)
    wg = AR.alloc("wg", [128, DC, 1024], BF16)
    wdd = AR.alloc("wdd1", [128, DC, 32], BF16)
    ghb = AR.alloc("ghb", [128, NH * HV], F32)

    RS = []
    for s in range(3):
        R = {
            "xt": AR.alloc(f"xt{s}", [128, DC, TT], F32), "xt_sem": newsem(f"s_xt{s}"),
            "hT": AR.alloc(f"hT{s}", [128, DC, TT], BF16), "hT_sem": newsem(f"s_hT{s}"),
            "ssb": AR.alloc(f"ssb{s}", [128, NH * HV], BF16), "ssb_sem": newsem(f"s_ssb{s}"), "v_sem": newsem(f"s_v{s}"),
            "sq": AR.alloc(f"sq{s}", [128, DC, TT], BF16), "rs": AR.alloc(f"rs{s}", [128, TT], F32), "tmp": AR.alloc(f"tmp{s}", [128, DC, TT], F32),
            "dda": AR.alloc(f"dda{s}", [33, TT], BF16),
            "nl": AR.alloc(f"nl{s}", [128, 512], F32), "cm": AR.alloc(f"cm{s}", [128, NH, TT], F32),
            "cmid": [AR.alloc(f"cmid{s}{d}", [128, NH], F32) for d in range(2)],
            "Eq": (lambda l: l * 2 if s == 2 else l + [AR.alloc(f"Eq{s}1", [128, NH, TT], F32)])([AR.alloc(f"Eq{s}0", [128, NH, TT], F32)]),
            "Ek": (lambda l: l * 2 if s == 2 else l + [AR.alloc(f"Ek{s}1", [128, NH, TT], F32)])([AR.alloc(f"Ek{s}0", [128, NH, TT], F32)]),
            "em": [AR.alloc(f"em{s}{d}", [128, NH], F32) for d in range(2)],
            "dd": [AR.alloc(f"dd{s}{d}", [128, NH], F32) for d in range(2)],
            "kvs": AR.alloc(f"kvs{s}", [128, NH, HV], F32),
            "ktT": (lambda l: l * 2 if s == 2 else l + [AR.alloc(f"ktT{s}1", [128, NH, TT], BF16)])([AR.alloc(f"ktT{s}0", [128, NH, TT], BF16)]),
            "ktm": AR.alloc(f"ktm{s}", [128, NH, HK], BF16),
            "v_sb": AR.alloc(f"v_sb{s}", [128, NH * HV], BF16),
        }
        op("dve", "memset", [], [R["dda"]], R["dda"].h[32:33, :], 1.0)
        RS.append(R)

    orderB = [("c", t, 0) for t in range(NCT)] + [("c", t, 1) for t in reversed(range(NCT))] + [("l", t, 1) for t in reversed(range(NT))]
    doneB = [False] * len(orderB)

    def genB(j, item, s):
        kind, t, d = item
        is_ctx = kind == "c"
        R = RS[s]
        bA, bB = 2 * s, 2 * s + 1
        xt, hT = R["xt"], R["hT"]
        dma("sp", xt.h[:, :, :], (ctx_t if is_ctx else x_t)[t], [], [xt], R["xt_sem"])
        yield
        yield from gen_hT(xt, hT, V_GS1C if is_ctx else V_GS1, V_SH1C if is_ctx else V_SH1, bA, R["sq"], R["rs"], R["tmp"])
        if not is_ctx:
            dma("sp", hT_s[t], hT.h[:, :, :], [hT], [hT_b[t]], R["hT_sem"])
        for c in range(DC):
            mm(psf(bB, TT)[0:32, :], wddB.h[:, c, :], hT.h[:, c, :], c == 0, c == DC - 1, [wddB, hT], [PS[bB]])
        op("act", "copy", [PS[bB]], [R["dda"]], out=R["dda"].h[0:32, :], in_=psf(bB, TT)[0:32, :])
        yield
        yield from decay_dir(R, d, bA, bB, False)
        yield
        for h in range(NH):
            for c in range(DC):
                mm(psf(bA, NH, TT)[:, h, :], wkB.h[:, c, h * HK:(h + 1) * HK], hT.h[:, c, :], c == 0, c == DC - 1, [wkB, hT], [PS[bA]])
        ktT = R["ktT"][d]
        op("dve", "tensor_tensor", [PS[bA], R["Ek"][d]], [ktT], out=ktT.h[:, :, :], in0=psf(bA, NH, TT), in1=R["Ek"][d].h[:, :, :], op=ALU.mult)
        yield
        v_sb = R["v_sb"]
        for blk, b in ((0, bB), (1, bA)):
            for c in range(DC):
                mm(psf(b, 512), hT.h[:, c, :], wvB.h[:, c, blk * 512:(blk + 1) * 512], c == 0, c == DC - 1, [hT, wvB], [PS[b]])
            op("act", "copy", [PS[b]], [v_sb], out=v_sb.h[:, blk * 512:(blk + 1) * 512], in_=psf(b, 512))
            yield
        if not is_ctx:
            dma("sp", v_s[t], v_sb.h[:, :], [v_sb], [v_b[t]], R["v_sem"])
        ktm = R["ktm"]
        for h in range(NH):
            op("pe", "transpose", [ktT, ident], [PS[bB]], psb(bB, NH, HK)[:, h, :], ktT.h[:, h, :], ident.h[:, :])
        op("dve", "tensor_copy", [PS[bB]], [ktm], out=ktm.h[:, :, :], in_=psb(bB, NH, HK))
        yield
        for h in range(NH):
            b = bA + h // 2
            mm(psf(b, 2, HV)[:, h % 2, :], ktm.h[:, h, :], v_sb.h[:, h * HV:(h + 1) * HV], True, True, [ktm, v_sb], [PS[b]])
        yield
        while j > 0 and not doneB[j - 1]:
            yield
        S = S_f if d == 0 else S_b
        if not is_ctx:
            sb_ = R["ssb"]
            for h in range(NH):
                op("act", "activation", [S, R["em"][d]], [sb_], out=sb_.h[:, h * HV:(h + 1) * HV], in_=S.h[:, h, :], func=AF.Identity, scale=R["em"][d].h[:, h:h + 1])
            dma("sp", ssb_s[t], sb_.h[:, :], [sb_], [ssb_b[t]], R["ssb_sem"])
        state_update(S, bA, R["Eq"][d], R["dd"][d], TT - 1 if d == 0 else 0, R["kvs"])
        doneB[j] = True
        yield

    s_w1 = {n: newsem("s_w1" + n) for n in ("q", "k", "v", "g", "dd", "ga", "go")}
    dma("sp", ghb.h[:, :], g_head.partition_broadcast(128), [], [ghb], newsem("s_ghb"))
    wload(wdd, w_in[:, C_DD:C_DD + 32], 32, s_w1["dd"])
    wload(wq, w_in[:, C_Q:C_Q + 512], 512, s_w1["q"])
    wload(wk, w_in[:, C_K:C_K + 512], 512, s_w1["k"])
    wload(wg, w_in[:, C_G:C_G + 1024], 1024, s_w1["g"])

    interleave(genB, orderB, 5, K=3)
    if dbg:
        dbg_t["sf"] = nc.dram_tensor("dbg_sf", [128, NH, HV], F32, kind="ExternalOutput").ap()
        dma("sp", dbg_t["sf"], S_f.h[:, :, :], [S_f], [], newsem("s_dbgsf"))
    AR.free(wkB, wvB, wddB, S_b)
    for R in RS:
        AR.free(R["xt"], R["sq"], R["rs"], R["tmp"])
    R = RS.pop()
    AR.free(R["hT"], R["ssb"], R["dda"], R["nl"], R["cm"], *R["cmid"], R["Eq"][0], R["Ek"][0], *R["em"], *R["dd"], R["kvs"], R["ktT"][0], R["ktm"], R["v_sb"])
    wga = AR.alloc("wga", [128, DC, 1024], BF16)
    wgo = AR.alloc("wgo", [128, DC, 1024], BF16)
    wload(wga, w_in[:, C_GA:C_GA + 1024], 1024, s_w1["ga"])
    wload(wgo, w_go, 1024, s_w1["go"])

    for s in range(2):
        R = RS[s]
        R["qtT"] = [AR.alloc(f"qtT{s}{d}", [128, NH, TT], BF16) for d in range(2)]
        R["scm"] = [AR.alloc(f"scm{s}{d}", [128, NH, TT], BF16) for d in range(2)]
        R["ssf"] = AR.alloc(f"ssf{s}", [128, NH, HV], BF16)
        R["sgg"] = AR.alloc(f"sgg{s}", [128, NH * HV], F32)
        R["ssq"] = AR.alloc(f"ssq{s}", [128, NH], F32)
        R["rso"] = AR.alloc(f"rso{s}", [128, NH], F32)
        R["og"] = AR.alloc(f"og{s}", [128, NH * HV], BF16)
        R["ogT"] = AR.alloc(f"ogT{s}", [128, DC, TT], BF16)
        R["th"] = AR.alloc(f"th{s}", [128, DC, TT], F32)
        R["ma"] = AR.alloc(f"ma{s}", [128, DC, TT], F32)
        R["ma_sem"] = newsem(f"s_ma{s}")
    doneF = [False] * NT

    op("dve", "tensor_scalar", [ghb], [ghb], out=ghb.h[:, :], in0=ghb.h[:, :], scalar1=0.5, scalar2=None, op0=ALU.mult)

    def decay_post(R, d, cb, cm):
        mid = 63 if d == 0 else 64
        last = TT - 1 if d == 0 else 0
        cmid, Eq, Ek, em, dd_ = R["cmid"][d], R["Eq"][d], R["Ek"][d], R["em"][d], R["dd"][d]
        op("dve", "tensor_copy", [PS[cb]], [cmid], out=cmid.h[:, :], in_=psf(cb, NH, TT)[:, :, mid])
        op("dve", "tensor_tensor", [PS[cb], cmid], [cm[1]], out=cm[0], in0=psf(cb, NH, TT), in1=cmid.h[:, :].unsqueeze(2).to_broadcast([128, NH, TT]), op=ALU.subtract)
        op("act", "activation", [cm[1]], [Eq], out=Eq.h[:, :, :], in_=cm[0], func=AF.Exp, scale=-1.0 / TAU)
        op("act", "activation", [cm[1]], [Ek], out=Ek.h[:, :, :], in_=cm[0], func=AF.Exp, scale=1.0 / TAU)
        op("act", "activation", [cmid], [em], out=em.h[:, :], in_=cmid.h[:, :], func=AF.Exp, scale=-1.0 / TAU)
        op("dve", "tensor_tensor", [em, Eq], [dd_], out=dd_.h[:, :], in0=em.h[:, :], in1=Eq.h[:, :, last], op=ALU.mult)

    def genF1(j, t, s):
        R = RS[s]
        b0, b1, b2, b3 = 4 * s, 4 * s + 1, 4 * s + 2, 4 * s + 3
        hT, dda, v_sb, ktm = R["hT"], R["dda"], R["v_sb"], R["ktm"]
        qtT, ktT, scm, sgg, th = R["qtT"], R["ktT"], R["scm"], R["sgg"], R["th"]
        nl = [(R["nl"].h[:, :], R["nl"]), (R["kvs"].h[:, 0:2, :].rearrange("p a b -> p (a b)"), R["kvs"])]
        cm = [(R["cm"].h[:, :, :], R["cm"]), (th.h[:, 0:4, :], th)]
        dma("sp", hT.h[:, :, :], hT_s[t], [hT_b[t]], [hT], R["hT_sem"])
        dma("sp", R["ssb"].h[:, :], ssb_s[t], [ssb_b[t]], [R["ssb"]], R["ssb_sem"])
        dma("sp", v_sb.h[:, :], v_s[t], [v_b[t]], [v_sb], R["v_sem"])
        yield
        for c in range(DC):
            mm(psf(b0, TT)[0:32, :], wdd.h[:, c, :], hT.h[:, c, :], c == 0, c == DC - 1, [wdd, hT], [PS[b0]])
        op("act", "copy", [PS[b0]], [dda], out=dda.h[0:32, :], in_=psf(b0, TT)[0:32, :])
        yield
        for d, zb in ((0, b1), (1, b2)):
            mm(psf(zb, 512), dda.h[:, :], wupt.h[:, d * 512:(d + 1) * 512], True, True, [dda, wupt], [PS[zb]])
            op("act", "activation", [PS[zb]], [nl[d][1]], out=nl[d][0], in_=psf(zb, 512), func=AF.Exp, scale=-1.0)
            op("act", "activation", [nl[d][1]], [nl[d][1]], out=nl[d][0], in_=nl[d][0], func=AF.Ln, bias=1.0)
        for (wt, b) in ((wq, b3), (wk, b0)):
            for h in range(NH):
                for c in range(DC):
                    mm(psf(b, NH, TT)[:, h, :], wt.h[:, c, h * HK:(h + 1) * HK], hT.h[:, c, :], c == 0, c == DC - 1, [wt, hT], [PS[b]])
            yield
        for d, cb in ((0, b1), (1, b2)):
            tri = trif if d == 0 else trib
            for h in range(NH):
                mm(psf(cb, NH, TT)[:, h, :], nl[d][0][:, h * HK:(h + 1) * HK], tri.h[:, :], True, True, [nl[d][1], tri], [PS[cb]])
            decay_post(R, d, cb, cm[d])
        yield
        for d in range(2):
            op("dve", "scalar_tensor_tensor", [PS[b3], R["Eq"][d]], [qtT[d]], out=qtT[d].h[:, :, :], in0=psf(b3, NH, TT), scalar=float(HK) ** -0.5,
               in1=R["Eq"][d].h[:, :, :], op0=ALU.mult, op1=ALU.mult)
            op("dve", "tensor_tensor", [PS[b0], R["Ek"][d]], [ktT[d]], out=ktT[d].h[:, :, :], in0=psf(b0, NH, TT), in1=R["Ek"][d].h[:, :, :], op=ALU.mult)
        yield
        for blk, b in ((0, b1), (1, b2)):
            for c in range(DC):
                mm(psf(b, 512), hT.h[:, c, :], wg.h[:, c, blk * 512:(blk + 1) * 512], c == 0, c == DC - 1, [hT, wg], [PS[b]])
            op("act", "activation", [PS[b]], [sgg], out=sgg.h[:, blk * 512:(blk + 1) * 512], in_=psf(b, 512), func=AF.Tanh, scale=0.5)
            op("dve", "scalar_tensor_tensor", [sgg, PS[b]], [sgg], out=sgg.h[:, blk * 512:(blk + 1) * 512], in0=sgg.h[:, blk * 512:(blk + 1) * 512], scalar=1.0,
               in1=psf(b, 512), op0=ALU.add, op1=ALU.mult)
        op("pool", "tensor_tensor", [sgg, ghb], [sgg], out=sgg.h[:, :], in0=sgg.h[:, :], in1=ghb.h[:, :], op=ALU.mult)
        yield
        for h in range(NH):
            op("pe", "transpose", [ktT[0], ident], [PS[b3]], psb(b3, NH, HK)[:, h, :], ktT[0].h[:, h, :], ident.h[:, :])
        op("dve", "tensor_copy", [PS[b3]], [ktm], out=ktm.h[:, :, :], in_=psb(b3, NH, HK))
        for h in range(NH):
            mm(psf(b0, NH, TT)[:, h, :], ktT[0].h[:, h, :], qtT[0].h[:, h, :], True, True, [ktT[0], qtT[0]], [PS[b0]])
        op("dve", "tensor_tensor", [PS[b0], trif], [scm[0]], out=scm[0].h[:, :, :], in0=psf(b0, NH, TT), in1=trif.h[:, :].unsqueeze(1).to_broadcast([128, NH, TT]), op=ALU.mult)
        yield
        for h in range(NH):
            mm(psf(b3, NH, TT)[:, h, :], ktT[1].h[:, h, :], qtT[1].h[:, h, :], True, True, [ktT[1], qtT[1]], [PS[b3]])
        op("dve", "tensor_tensor", [PS[b3], trib], [scm[1]], out=scm[1].h[:, :, :], in0=psf(b3, NH, TT), in1=trib.h[:, :].unsqueeze(1).to_broadcast([128, NH, TT]), op=ALU.mult)
        yield
        while j > 0 and not doneF[j - 1]:
            yield
        ssf = R["ssf"]
        op("pool", "tensor_tensor", [S_f, R["em"][0]], [ssf], out=ssf.h[:, :, :], in0=S_f.h[:, :, :], in1=R["em"][0].h[:, :].unsqueeze(2).to_broadcast([128, NH, HV]), op=ALU.mult)
        for h in range(NH):
            b = b1 + h // 2
            mm(psf(b, 2, HV)[:, h % 2, :], ktm.h[:, h, :], v_sb.h[:, h * HV:(h + 1) * HV], True, True, [ktm, v_sb], [PS[b]])
        state_update(S_f, b1, R["Eq"][0], R["dd"][0], TT - 1, R["kvs"])
        doneF[j] = True
        yield
        for h in range(NH):
            b = b0 if h < 2 else b3
            oap = psf(b, 2, HV)[:, h % 2, :]
            vh = v_sb.h[:, h * HV:(h + 1) * HV]
            mm(oap, scm[0].h[:, h, :], vh, True, False, [scm[0], v_sb], [PS[b]])
            mm(oap, scm[1].h[:, h, :], vh, False, False, [scm[1], v_sb], [PS[b]])
            mm(oap, qtT[0].h[:, h, :], ssf.h[:, h, :], False, False, [qtT[0], ssf], [PS[b]])
            mm(oap, qtT[1].h[:, h, :], R["ssb"].h[:, h * HV:(h + 1) * HV], False, True, [qtT[1], R["ssb"]], [PS[b]])
        osq, ssq, rso, og = R["kvs"], R["ssq"], R["rso"], R["og"]
        for hh, b in ((0, b0), (1, b3)):
            op("act", "activation", [PS[b]], [osq], out=osq.h[:, hh * 2:(hh + 1) * 2, :], in_=psf(b, 2, HV), func=AF.Square)
        yield
        for co in range(DC):
            b = b1 + co // 4
            for c in range(DC):
                mm(psf(b, 4, TT)[:, co % 4, :], wga.h[:, c, co * 128:(co + 1) * 128], hT.h[:, c, :], c == 0, c == DC - 1, [wga, hT], [PS[b]])
            if co % 4 == 3:
                hh = co // 4
                op("act", "activation", [PS[b]], [th], out=th.h[:, hh * 4:(hh + 1) * 4, :], in_=psf(b, 4, TT), func=AF.Tanh, scale=0.5)
        op("dve", "reduce_sum", [osq], [ssq], out=ssq.h[:, :], in_=osq.h[:, :, :], axis=mybir.AxisListType.X)
        op("act", "activation", [ssq], [rso], out=rso.h[:, :], in_=ssq.h[:, :], func=AF.Ln, scale=1.0 / HV, bias=EPS)
        op("act", "activation", [rso], [rso], out=rso.h[:, :], in_=rso.h[:, :], func=AF.Exp, scale=-0.5)
        yield
        for h in range(NH):
            b = b0 if h < 2 else b3
            op("dve", "scalar_tensor_tensor", [PS[b], rso, sgg], [og], out=og.h[:, h * HV:(h + 1) * HV], in0=psf(b, 2, HV)[:, h % 2, :], scalar=rso.h[:, h:h + 1],
               in1=sgg.h[:, h * HV:(h + 1) * HV], op0=ALU.mult, op1=ALU.mult)
        yield
        ogT = R["ogT"]
        for c in range(DC):
            op("pe", "transpose", [og, ident], [PS[b0]], psb(b0, DC, TT)[:, c, :], og.h[:, c * 128:(c + 1) * 128], ident.h[:, :])
        op("act", "copy", [PS[b0]], [ogT], out=ogT.h[:, :, :], in_=psb(b0, DC, TT))
        yield
        ma = R["ma"]
        for co in range(DC):
            b = b3 if co < 4 else b1
            for c in range(DC):
                mm(psf(b, 4, TT)[:, co % 4, :], wgo.h[:, c, co * 128:(co + 1) * 128], ogT.h[:, c, :], c == 0, c == DC - 1, [wgo, ogT], [PS[b]])
            if co % 4 == 3:
                hh = co // 4
                op("dve", "scalar_tensor_tensor", [th, PS[b]], [ma], out=ma.h[:, hh * 4:(hh + 1) * 4, :], in0=th.h[:, hh * 4:(hh + 1) * 4, :], scalar=1.0,
                   in1=psf(b, 4, TT), op0=ALU.add, op1=ALU.mult)
                yield
        dma("sp", ma_s[t], ma.h[:, :, :], [ma], [ma_b[t]], R["ma_sem"])
        yield

    interleave(genF1, list(range(NT)), 9)
    AR.free(wq, wk, wg, wdd, wga, wgo, ghb, S_f, wupt)
    for R in RS:
        AR.free(R["ssb"], R["dda"], R["nl"], R["cm"], *R["cmid"], *R["Eq"], *R["Ek"], *R["em"], *R["dd"], R["kvs"], *R["ktT"], R["ktm"], R["v_sb"],
                *R["qtT"], *R["scm"], R["ssf"], R["sgg"], R["ssq"], R["rso"], R["og"], R["ogT"], R["th"])

    wcb = AR.alloc("wcb", [128, DC, 1024], BF16)
    wcc = AR.alloc("wcc", [128, DC, 1024], BF16)
    wch = AR.alloc("wch", [128, DC, 1024], BF16)
    wgb = AR.alloc("wgb", [128, DC, 1024], BF16)
    wco = AR.alloc("wco", [128, DC, 1024], BF16)
    wmo = AR.alloc("wmo", [128, DC, 1024], BF16)
    wload(wcc, w_in[:, C_CC:C_CC + 1024], 1024, newsem("s_w2cc"))
    wload(wch, w_in[:, C_CH:C_CH + 1024], 1024, newsem("s_w2ch"))
    wload(wcb, w_in[:, C_CB:C_CB + 1024], 1024, newsem("s_w2cb"))
    wload(wgb, w_in[:, C_GB:C_GB + 1024], 1024, newsem("s_w2gb"))
    wload(wco, w_co, 1024, newsem("s_w2co"))
    wload(wmo, w_mo, 1024, newsem("s_w2mo"))
    for s in range(2):
        R = RS[s]
        R["xt"] = AR.alloc(f"xtb{s}", [128, DC, TT], F32)
        for n in ("cch", "u", "uc", "cbs", "thb", "mrg", "t1"):
            R[n] = AR.alloc(f"{n}{s}", [128, DC, TT], F32)
        R["x1"] = x1_pre[s]
        R["ycin"] = AR.alloc(f"ycin{s}", [128, DC, TT], BF16)
        R["mrgb"] = AR.alloc(f"mrgb{s}", [128, DC, TT], BF16)
        R["x1_sem"] = newsem(f"s_x1{s}")

    def feat_mm(wt, rhs_t, bpair, evac):
        for co in range(DC):
            b = bpair + co // 4
            for c in range(DC):
                mm(psf(b, 4, TT)[:, co % 4, :], wt.h[:, c, co * 128:(co + 1) * 128], rhs_t.h[:, c, :], c == 0, c == DC - 1, [wt, rhs_t], [PS[b]])
            if co % 4 == 3:
                evac(co // 4, b)

    def wcv_bc(k):
        return wcv.h[:, :, k:k + 1]

    def genF2(j, t, s):
        R = RS[s]
        b0 = 4 * s
        hT, xt, ma = R["hT"], R["xt"], R["ma"]
        cch, u, uc, cbs, thb, mrg, x1t, t1, ycin, mrgb = (R[n] for n in ("cch", "u", "uc", "cbs", "thb", "mrg", "x1", "t1", "ycin", "mrgb"))
        dma("sp", hT.h[:, :, :], hT_s[t], [hT_b[t]], [hT], R["hT_sem"])
        dma("sp", ma.h[:, :, :], ma_s[t], [ma_b[t]], [ma], R["ma_sem"])
        dma("sp", xt.h[:, :, :], x_t[t], [], [xt], R["xt_sem"])
        yield
        sl = lambda hh: slice(hh * 4, (hh + 1) * 4)
        feat_mm(wcc, hT, b0, lambda hh, b: op("act", "copy", [PS[b]], [cch], out=cch.h[:, sl(hh), :], in_=psf(b, 4, TT)))
        yield
        feat_mm(wch, hT, b0 + 2, lambda hh, b: op("dve", "tensor_tensor", [PS[b], cch], [u], out=u.h[:, sl(hh), :], in0=psf(b, 4, TT), in1=cch.h[:, sl(hh), :], op=ALU.mult))
        yield
        feat_mm(wcb, hT, b0, lambda hh, b: op("act", "copy", [PS[b]], [cbs], out=cbs.h[:, sl(hh), :], in_=psf(b, 4, TT)))
        yield
        feat_mm(wgb, hT, b0 + 2, lambda hh, b: op("act", "activation", [PS[b]], [thb], out=thb.h[:, sl(hh), :], in_=psf(b, 4, TT), func=AF.Tanh, scale=0.5))
        yield
        u4 = u.h[:, :, :].rearrange("p c (r w) -> p c r w", w=64)
        uc4 = uc.h[:, :, :].rearrange("p c (r w) -> p c r w", w=64)
        t14 = t1.h[:, :, :].rearrange("p c (r w) -> p c r w", w=64)
        op("dve", "tensor_tensor", [u, wcv], [uc], out=uc.h[:, :, :], in0=u.h[:, :, :], in1=wcv.h[:, :, 1:2].to_broadcast([128, DC, TT]), op=ALU.mult)
        op("pool", "tensor_tensor", [u, wcv], [t1], out=t14[:, :, :, 1:64], in0=u4[:, :, :, 0:63], in1=wcv.h[:, :, 0:1].unsqueeze(3).to_broadcast([128, DC, 2, 63]), op=ALU.mult)
        yield
        op("dve", "tensor_tensor", [uc, t1], [uc], out=uc4[:, :, :, 1:64], in0=uc4[:, :, :, 1:64], in1=t14[:, :, :, 1:64], op=ALU.add)
        op("pool", "tensor_tensor", [u, wcv, t1], [t1], out=t14[:, :, :, 0:63], in0=u4[:, :, :, 1:64], in1=wcv.h[:, :, 2:3].unsqueeze(3).to_broadcast([128, DC, 2, 63]), op=ALU.mult)
        yield
        op("dve", "tensor_tensor", [uc, t1], [uc], out=uc4[:, :, :, 0:63], in0=uc4[:, :, :, 0:63], in1=t14[:, :, :, 0:63], op=ALU.add)
        op("dve", "tensor_tensor", [cbs, uc], [ycin], out=ycin.h[:, :, :], in0=cbs.h[:, :, :], in1=uc.h[:, :, :], op=ALU.mult)
        yield

        def ev_co(hh, b):
            op("dve", "scalar_tensor_tensor", [thb, PS[b]], [mrg], out=mrg.h[:, sl(hh), :], in0=thb.h[:, sl(hh), :], scalar=1.0, in1=psf(b, 4, TT), op0=ALU.add, op1=ALU.mult)
            op("pool", "tensor_tensor", [mrg, ma], [mrg], out=mrg.h[:, sl(hh), :], in0=mrg.h[:, sl(hh), :], in1=ma.h[:, sl(hh), :], op=ALU.add)
            op("act", "activation", [mrg], [mrgb], out=mrgb.h[:, sl(hh), :], in_=mrg.h[:, sl(hh), :], func=AF.Identity, scale=0.5)
        feat_mm(wco, ycin, b0, ev_co)
        yield

        def ev_mo(hh, b):
            op("dve", "tensor_tensor", [PS[b], vec], [t1], out=t1.h[:, sl(hh), :], in0=psf(b, 4, TT), in1=vec.h[:, V_GT1, hh * 4:(hh + 1) * 4].unsqueeze(2).to_broadcast([128, 4, TT]), op=ALU.mult)
            op("pool", "tensor_tensor", [t1, xt], [x1t], out=x1t.h[:, sl(hh), :], in0=t1.h[:, sl(hh), :], in1=xt.h[:, sl(hh), :], op=ALU.add)
        feat_mm(wmo, mrgb, b0 + 2, ev_mo)
        yield
        dma("sp", x1_s[t], x1t.h[:, :, :], [x1t], [x1_b[t]], R["x1_sem"])
        yield

    interleave(genF2, list(range(NT)), 6)
    AR.free(wcb, wcc, wch, wgb, wco, wmo, wcv)
    for R in RS:
        AR.free(R["hT"], R["ma"], R["xt"], R["cch"], R["u"], R["uc"], R["cbs"], R["thb"], R["mrg"], R["t1"], R["ycin"], R["mrgb"])

    NG = (FC + 3) // 4
    wfi_g, wfo_g = [], []
    wst = AR.alloc("wst", [128, DC, 512], F32)
    s_wst = newsem("s_wst")
    late_w = {}
    for g in range(NG):
        nf = min(4, FC - g * 4)
        wi = AR.alloc(f"wfi{g}", [128, DC, 2 * nf * 128], BF16)
        wo = AR.alloc(f"wfo{g}", [128, nf, D], BF16)
        blocks = [(wi.h[:, :, 0:nf * 128], wi, wst.h[:, :, 0:nf * 128], w_fi[:, g * 512:g * 512 + nf * 128].rearrange("(c p) n -> p c n", p=128)),
                  (wi.h[:, :, nf * 128:2 * nf * 128], wi, wst.h[:, :, 0:nf * 128], w_fi[:, DFF + g * 512:DFF + g * 512 + nf * 128].rearrange("(c p) n -> p c n", p=128))]
        for hh in range(2):
            blocks.append((wo.h[:, :, hh * 512:(hh + 1) * 512], wo, wst.h[:, 0:nf, :],
                           w_fo[g * 512:g * 512 + nf * 128, hh * 512:(hh + 1) * 512].rearrange("(c p) n -> p c n", p=128)))
        if g % 2 == 0:
            si, so = newsem(f"s_wfi{g}"), newsem(f"s_wfo{g}")
            for (dst, dt_, _st, src_) in blocks:
                dma("pool", dst, src_, [], [dt_], si if dt_ is wi else so)
        else:
            late_w[g] = blocks
        wfi_g.append(wi)
        wfo_g.append(wo)
    for s in range(2):
        R = RS[s]
        R["sq"] = AR.alloc(f"sq3{s}", [128, DC, TT], BF16)
        R["rs"] = AR.alloc(f"rs3{s}", [128, TT], F32)
        R["tmp"] = AR.alloc(f"tmp3{s}", [128, DC, TT], F32)
        R["h2"] = AR.alloc(f"h2{s}", [128, DC, TT], BF16)
        R["sa"] = AR.alloc(f"sa{s}", [128, 4, TT], F32)
        R["hff"] = [AR.alloc(f"hff{s}{i}", [128, 4, TT], BF16) for i in range(2)]
        R["x2"] = AR.alloc(f"x2{s}", [128, DC, TT], F32)
        R["sqb"] = AR.alloc(f"sqb{s}", [128, DC, TT], BF16)
        R["rsb"] = AR.alloc(f"rsb{s}", [128, TT], F32)
        R["ot_sem"] = newsem(f"s_ot{s}")
    ot_sems = [RS[0]["ot_sem"], RS[1]["ot_sem"]]

    def genF3(j, t, s):
        R = RS[s]
        b0 = 4 * s
        x1t, h2, x2, sa = R["x1"], R["h2"], R["x2"], R["sa"]
        dma("sp", x1t.h[:, :, :], x1_s[t], [x1_b[t]], [x1t], R["x1_sem"])
        yield
        yield from gen_hT(x1t, h2, V_GS2, V_SH2, b0, R["sq"], R["rs"], R["tmp"])
        def out_proj(g):
            nf = min(4, FC - g * 4)
            wo, hf = wfo_g[g], R["hff"][g % 2]
            for ff in range(nf):
                f = g * 4 + ff
                for co in range(DC):
                    b = b0 + 2 + co // 4
                    op("pe", "matmul", [wo, hf], [PS[b]], psf(b, 4, TT)[:, co % 4, :], lhsT=wo.h[:, ff, co * 128:(co + 1) * 128], rhs=hf.h[:, ff, :],
                       start=(f == 0 and co % 4 == 0), stop=(f == FC - 1), skip_group_check=True)

        for g in range(NG):
            nf = min(4, FC - g * 4)
            wi, hf = wfi_g[g], R["hff"][g % 2]
            if j == 0 and g in late_w:
                for (dst, dt_, st_, src_) in late_w[g]:
                    dma("sp", st_, src_, [], [wst], s_wst)
                    op("dve", "tensor_copy", [wst], [dt_], out=dst, in_=st_)
            for (b, base) in ((b0, 0), (b0 + 1, nf * 128)):
                for ff in range(nf):
                    for c in range(DC):
                        mm(psf(b, 4, TT)[:, ff, :], wi.h[:, c, base + ff * 128: base + (ff + 1) * 128], h2.h[:, c, :], c == 0, c == DC - 1, [wi, h2], [PS[b]])
            op("act", "activation", [PS[b0]], [sa], out=sa.h[:, 0:nf, :], in_=psf(b0, 4, TT)[:, 0:nf, :], func=AF.Silu)
            op("dve", "tensor_tensor", [PS[b0 + 1], sa], [hf], out=hf.h[:, 0:nf, :], in0=psf(b0 + 1, 4, TT)[:, 0:nf, :], in1=sa.h[:, 0:nf, :], op=ALU.mult)
            yield
            if g > 0:
                out_proj(g - 1)
                yield
        out_proj(NG - 1)
        yield
        for hh in range(2):
            b = b0 + 2 + hh
            op("dve", "tensor_tensor", [PS[b], vec], [x2], out=x2.h[:, hh * 4:(hh + 1) * 4, :], in0=psf(b, 4, TT), in1=vec.h[:, V_GT2, hh * 4:(hh + 1) * 4].unsqueeze(2).to_broadcast([128, 4, TT]), op=ALU.mult)
        op("pool", "tensor_tensor", [x2, x1t], [x2], out=x2.h[:, :, :], in0=x2.h[:, :, :], in1=x1t.h[:, :, :], op=ALU.add)
        yield
        sq, rs = R["sqb"], R["rsb"]
        op("act", "activation", [x2], [sq], out=sq.h[:, :, :], in_=x2.h[:, :, :], func=AF.Square)
        yield
        for c in range(DC):
            mm(psf(b0, TT), ones.h[:, :], sq.h[:, c, :], c == 0, c == DC - 1, [ones, sq], [PS[b0]])
        yield
        op("act", "activation", [PS[b0]], [rs], out=rs.h[:, :], in_=psf(b0, TT), func=AF.Ln, scale=1.0 / D, bias=EPS)
        op("act", "activation", [rs], [rs], out=rs.h[:, :], in_=rs.h[:, :], func=AF.Exp, scale=-0.5)
        o_ = R["tmp"]
        op("pool", "tensor_tensor", [x2, rs], [o_], out=o_.h[:, :, :], in0=x2.h[:, :, :], in1=rs.h[:, :].unsqueeze(1).to_broadcast([128, DC, TT]), op=ALU.mult)
        op("dve", "tensor_tensor", [o_, vec], [o_], out=o_.h[:, :, :], in0=o_.h[:, :, :], in1=bc_c(vslot(V_GFIN), TT), op=ALU.mult)
        dma("sp", out_t[t], o_.h[:, :, :], [o_], [], R["ot_sem"])
        yield

    interleave(genF3, list(range(NT)), 10)

    def final_waits():
        return [(s_, 16 * P.dma_counts[id(s_)]) for s_ in ot_sems if id(s_) in P.dma_counts]

    P.emit(esems, final_waits)
    return nc, P


def _tile_fm(a):
    T = a.shape[0]
    return np.ascontiguousarray(a.reshape(T // TT, TT, DC, 128).transpose(0, 3, 2, 1))


def _untile_fm(a):
    nt = a.shape[0]
    return np.ascontiguousarray(a.transpose(0, 3, 2, 1).reshape(nt * TT, D))


def _fm(v):
    return np.ascontiguousarray(v.reshape(DC, 128).T)


def make_in_maps(x, c, ctx, c_ctx, w_ada, b_ada, g_mix, w_in, w_dec_up, b_dec, g_head, w_gla_out,
                 w_conv, w_conv_out, w_mix_out, g_ffn, w_ffn_in, w_ffn_out, g_final):
    f = lambda a: np.ascontiguousarray(np.asarray(a, dtype=np.float32))
    x, c, ctx, c_ctx = f(x), f(c), f(ctx), f(c_ctx)
    wup = np.zeros((33, 1024), np.float32)
    wup[0:16, 0:512] = f(w_dec_up)[0, 0]
    wup[16:32, 512:1024] = f(w_dec_up)[0, 1]
    wup[32, 0:512] = f(b_dec)[0, 0]
    wup[32, 512:1024] = f(b_dec)[0, 1]
    tri = np.triu(np.ones((128, 128), np.float32))
    shared = {
        "w_ada": f(w_ada)[0], "b_ada_fm": np.ascontiguousarray(f(b_ada)[0].reshape(48, 128).T),
        "g_mix_fm": _fm(f(g_mix)[0]), "g_ffn_fm": _fm(f(g_ffn)[0]), "g_fin_fm": _fm(f(g_final)),
        "w_in": f(w_in)[0], "wup_aug": wup, "g_head4": np.ascontiguousarray(np.tile(f(g_head)[0], NH)[None, :]),
        "w_gla_out": f(w_gla_out)[0], "w_conv_out": f(w_conv_out)[0], "w_mix_out": f(w_mix_out)[0],
        "w_conv_fm": np.ascontiguousarray(f(w_conv)[0].reshape(3, DC, 128).transpose(2, 1, 0)),
        "w_ffn_in": f(w_ffn_in)[0], "w_ffn_out": f(w_ffn_out)[0],
        "c_ident": np.eye(128, dtype=np.float32), "c_trif": tri, "c_trib": np.ascontiguousarray(tri.T),
    }
    maps = []
    for b in range(x.shape[0]):
        m = dict(shared)
        m["x_t"] = _tile_fm(x[b])
        m["ctx_t"] = _tile_fm(ctx[b])
        m["cvec"] = np.ascontiguousarray(np.stack([_fm(c[b]), _fm(c_ctx)], axis=-1))
        maps.append(m)
    return maps


_CACHE = {}


def kernel(**inputs):
    x = np.asarray(inputs["x"])
    B, T, _ = x.shape
    NT = T // TT
    NCT = np.asarray(inputs["ctx"]).shape[1] // TT
    key = (NT, NCT)
    if key not in _CACHE:
        _CACHE[key] = build_program(NT, NCT)[0]
    nc = _CACHE[key]
    in_maps = make_in_maps(**inputs)
    res = run_bass_kernel_spmd(nc, in_maps, core_ids=list(range(B)))
    out = np.stack([_untile_fm(np.asarray(r["out_t"])) for r in res.results], axis=0)
    return out.astype(np.float32)
```
